# Optimizing a Trainium2 kernel written in Bass

```python
import jax, jax.numpy as jnp
from jax import lax
import numpy as np

D_MODEL = 1024
BATCH = 4
SEQ = 4096
DEPTH = 1

CHUNK = 64
N_META = 16
Q_BLOCK = 128
EPS = 1e-6
NEG_INF = -1e30
ROPE_BASE = 10000.0

FOX_HEADS = 8
FOX_HEAD_DIM = D_MODEL // (2 * FOX_HEADS)
FOX_WIDTH = FOX_HEADS * FOX_HEAD_DIM
MLA_HEADS = 8
MLA_NOPE_DIM = 64
MLA_ROPE_DIM = 32
MLA_V_DIM = D_MODEL // (2 * MLA_HEADS)
MLA_Q_RANK = D_MODEL // 4
MLA_KV_RANK = D_MODEL // 8
MLA_QK_DIM = MLA_NOPE_DIM + MLA_ROPE_DIM
MLA_WIDTH = MLA_HEADS * MLA_V_DIM
D_MIX = FOX_WIDTH + MLA_WIDTH

IN_SPLITS = (FOX_WIDTH, FOX_WIDTH, FOX_WIDTH, FOX_HEADS, FOX_WIDTH,
             MLA_Q_RANK, MLA_KV_RANK, MLA_ROPE_DIM, MLA_WIDTH)
D_IN = sum(IN_SPLITS)

kernel_name = "hybrid_fox_mla_parallel_heads"


def _rmsnorm(x, g):
    xf = x.astype(jnp.float32)
    r = lax.rsqrt(jnp.mean(xf * xf, axis=-1, keepdims=True) + EPS)
    return (xf * r * g.astype(jnp.float32)).astype(x.dtype)


def _rope(x, cos, sin):
    half = x.shape[-1] // 2
    x1, x2 = x[..., :half], x[..., half:]
    c, s = cos[None, :, None, :], sin[None, :, None, :]
    return jnp.concatenate([x1 * c - x2 * s, x1 * s + x2 * c], axis=-1)


def _chunk_id(p):
    return jnp.where(p < N_META, 0, 1 + (p - N_META) // CHUNK)


def _chunk_end(p):
    if p < N_META:
        return N_META
    return N_META + ((p - N_META) // CHUNK + 1) * CHUNK


def _block_sweep(q, k, v, scale, key_end, mask_and_bias):
    L = q.shape[1]
    outs = []
    for q0 in range(0, L, Q_BLOCK):
        kend = key_end(q0)
        s = jnp.einsum('bqhd,bkhd->bhqk', q[:, q0:q0 + Q_BLOCK], k[:, :kend]).astype(jnp.float32) * scale
        s = mask_and_bias(s, q0, kend)
        p = jax.nn.softmax(s, axis=-1).astype(v.dtype)
        outs.append(jnp.einsum('bhqk,bkhd->bqhd', p, v[:, :kend]))
    return jnp.concatenate(outs, axis=1)


def _layer(x, norm_pre, norm_post, w_in, b_f, q_norm, w_uq, kv_norm, w_ukv, w_out, cos, sin):
    B, L, _ = x.shape
    h = _rmsnorm(x, norm_pre)
    proj = h @ w_in
    (fq, fk, fv, f_logit, f_gate, c_q, c_kv, k_r, m_gate) = jnp.split(
        proj, np.cumsum(IN_SPLITS)[:-1].tolist(), axis=-1)

    fq = fq.reshape(B, L, FOX_HEADS, FOX_HEAD_DIM)
    fk = fk.reshape(B, L, FOX_HEADS, FOX_HEAD_DIM)
    fv = fv.reshape(B, L, FOX_HEADS, FOX_HEAD_DIM)
    log_f = jax.nn.log_sigmoid((f_logit + b_f).astype(jnp.float32))
    cum = jnp.cumsum(log_f, axis=1).transpose(0, 2, 1)

    def fox_mask_bias(s, q0, kend):
        qpos = q0 + jnp.arange(Q_BLOCK)
        kpos = jnp.arange(kend)
        decay = cum[:, :, q0:q0 + Q_BLOCK, None] - cum[:, :, None, :kend]
        return jnp.where(kpos[None, :] <= qpos[:, None], s + decay, NEG_INF)

    fox = _block_sweep(fq, fk, fv, FOX_HEAD_DIM ** -0.5,
                       lambda q0: q0 + Q_BLOCK, fox_mask_bias)
    fox = fox.reshape(B, L, FOX_WIDTH) * jax.nn.silu(f_gate)

    q = (_rmsnorm(c_q, q_norm) @ w_uq).reshape(B, L, MLA_HEADS, MLA_QK_DIM)
    q = jnp.concatenate([q[..., :MLA_NOPE_DIM], _rope(q[..., MLA_NOPE_DIM:], cos, sin)], axis=-1)
    kv = (_rmsnorm(c_kv, kv_norm) @ w_ukv).reshape(B, L, MLA_HEADS, MLA_NOPE_DIM + MLA_V_DIM)
    k_nope, mv = kv[..., :MLA_NOPE_DIM], kv[..., MLA_NOPE_DIM:]
    k_rope = _rope(k_r[:, :, None, :], cos, sin)
    k = jnp.concatenate([k_nope, jnp.broadcast_to(k_rope, (B, L, MLA_HEADS, MLA_ROPE_DIM))], axis=-1)

    def mla_mask(s, q0, kend):
        cq = _chunk_id(q0 + jnp.arange(Q_BLOCK))
        ck = _chunk_id(jnp.arange(kend))
        return jnp.where(ck[None, :] <= cq[:, None], s, NEG_INF)

    mla = _block_sweep(q, k, mv, MLA_QK_DIM ** -0.5,
                       lambda q0: min(L, _chunk_end(q0 + Q_BLOCK - 1)), mla_mask)
    mla = mla.reshape(B, L, MLA_WIDTH) * jax.nn.silu(m_gate)

    y = jnp.concatenate([fox, mla], axis=-1) @ w_out
    return x + _rmsnorm(y, norm_post)


def setup_inputs(seed: int = 0) -> dict:
    key = jax.random.key(seed)
    ks = jax.random.split(key, 14)
    n = jax.random.normal
    f32 = jnp.float32
    return {
        "x": n(ks[0], (BATCH, SEQ, D_MODEL), f32),
        "meta": n(ks[1], (N_META, D_MODEL), f32),
        "norm_pre": 1.0 + 0.05 * n(ks[2], (DEPTH, D_MODEL), f32),
        "norm_post": 1.0 + 0.05 * n(ks[3], (DEPTH, D_MODEL), f32),
        "w_in": n(ks[4], (DEPTH, D_MODEL, D_IN), f32) * D_MODEL ** -0.5,
        "b_f": 2.0 + 0.5 * n(ks[5], (DEPTH, FOX_HEADS), f32),
        "q_norm": 1.0 + 0.05 * n(ks[6], (DEPTH, MLA_Q_RANK), f32),
        "w_uq": n(ks[7], (DEPTH, MLA_Q_RANK, MLA_HEADS * MLA_QK_DIM), f32) * MLA_Q_RANK ** -0.5,
        "kv_norm": 1.0 + 0.05 * n(ks[8], (DEPTH, MLA_KV_RANK), f32),
        "w_ukv": n(ks[9], (DEPTH, MLA_KV_RANK, MLA_HEADS * (MLA_NOPE_DIM + MLA_V_DIM)), f32) * MLA_KV_RANK ** -0.5,
        "w_out": n(ks[10], (DEPTH, D_MIX, D_MODEL), f32) * D_MIX ** -0.5,
    }


def reference(x, meta, norm_pre, norm_post, w_in, b_f, q_norm, w_uq, kv_norm, w_ukv, w_out):
    B, S, D = x.shape
    L = S + N_META
    Lp = ((L + Q_BLOCK - 1) // Q_BLOCK) * Q_BLOCK
    h = jnp.concatenate([
        jnp.broadcast_to(meta.astype(x.dtype)[None], (B, N_META, D)),
        x,
        jnp.zeros((B, Lp - L, D), x.dtype)], axis=1)

    pos = jnp.arange(Lp, dtype=jnp.float32)
    inv_freq = ROPE_BASE ** (-jnp.arange(0, MLA_ROPE_DIM, 2, dtype=jnp.float32) / MLA_ROPE_DIM)
    ang = pos[:, None] * inv_freq[None, :]
    cos, sin = jnp.cos(ang).astype(x.dtype), jnp.sin(ang).astype(x.dtype)

    for l in range(DEPTH):
        h = _layer(h, norm_pre[l], norm_post[l], w_in[l], b_f[l], q_norm[l], w_uq[l],
                   kv_norm[l], w_ukv[l], w_out[l], cos, sin)
    return h[:, N_META:N_META + S]
```

```python
import os
import numpy as np
import ml_dtypes
from contextlib import ExitStack
import concourse.bass as bass
import concourse.mybir as mybir
from concourse.bass_utils import run_bass_kernel_spmd

F32 = mybir.dt.float32
BF16 = mybir.dt.bfloat16
AF = mybir.ActivationFunctionType
ALU = mybir.AluOpType

D = 1024
S = 4096
NB = 33
TA = NB * 128
NOWN = 2048
EPS = 1e-6
NEG = -30000.0
FOX_SCALE = 64 ** -0.5
MLA_SCALE = 96 ** -0.5
TILES = [(t * 512, min(512, TA - t * 512)) for t in range((TA + 511) // 512)]

O_FQ, O_FK, O_FV, O_FL, O_FG, O_CQ, O_CKV, O_KR, O_MG = 0, 512, 1024, 1536, 1544, 2056, 2312, 2440, 2472


class Res:
    __slots__ = ("name", "lw", "rd")

    def __init__(self, name):
        self.name = name
        self.lw = None
        self.rd = {}


class Builder:
    def __init__(self, nc, es):
        self.nc = nc
        self.E = {"pe": nc.tensor, "act": nc.scalar, "dve": nc.vector, "pool": nc.gpsimd, "sp": nc.sync}
        self.sem = {}
        self.cnt = {}
        for e in ("pe", "act", "dve", "pool"):
            self.sem[e] = es.enter_context(nc.semaphore("s_" + e))
            self.cnt[e] = 0
        self.es = es
        self.known = {e: {} for e in self.E}
        self.snap = {}
        self.nwaits = 0

    NSLOT = 40

    def stream(self, name):
        pass

    def init_dma_slots(self):
        self.dma_i = 0
        for i in range(self.NSLOT):
            k = "dma%d" % i
            self.sem[k] = self.es.enter_context(self.nc.semaphore("d_%d" % i))
            self.cnt[k] = 0

    def _wait(self, eng, deps):
        kn = self.known[eng]
        best = {}
        for (e, v) in deps:
            if v > best.get(e, 0):
                best[e] = v
        for e, v in best.items():
            if kn.get(e, 0) >= v:
                continue
            assert v <= self.cnt[e], (eng, e, v, self.cnt[e])
            self.E[eng].wait_ge(self.sem[e], v)
            self.nwaits += 1
            kn[e] = v
            sn = self.snap.get((e, v))
            if sn:
                for k2, v2 in sn.items():
                    if kn.get(k2, 0) < v2:
                        kn[k2] = v2

    def _deps(self, eng, reads, writes):
        deps = []
        for r in reads:
            if r.lw is not None:
                if not (r.lw[0] == eng and eng == "pe"):
                    deps.append(r.lw)
        for w in writes:
            if w.lw is not None and not (w.lw[0] == eng and eng == "pe"):
                deps.append(w.lw)
            for e, v in w.rd.items():
                if not (e == eng and eng == "pe"):
                    deps.append((e, v))
        return deps

    def _commit(self, ev, reads, writes):
        for w in writes:
            w.lw = ev
            w.rd = {}
        for r in reads:
            if r.rd.get(ev[0], 0) < ev[1]:
                r.rd[ev[0]] = ev[1]

    def op(self, eng, fn, reads=(), writes=(), signal=True, late=(), late_w=()):
        self._wait(eng, self._deps(eng, reads, writes))
        ldeps = self._deps(eng, late, late_w)
        kn = self.known[eng]
        best = {}
        for (e, v) in ldeps:
            if v > best.get(e, 0) and kn.get(e, 0) < v:
                best[e] = v
        attach = None
        if len(best) == 1:
            attach = list(best.items())[0]
        elif len(best) > 1:
            self._wait(eng, ldeps)
        inst = fn()
        if attach is not None:
            e, v = attach
            assert v <= self.cnt[e], (eng, e, v, self.cnt[e])
            inst._wait_ge(self.sem[e], v)
            self.nwaits += 1
            kn[e] = v
            sn = self.snap.get((e, v))
            if sn:
                for k2, v2 in sn.items():
                    if kn.get(k2, 0) < v2:
                        kn[k2] = v2
        n = self.cnt[eng] + 1
        if signal:
            inst.then_inc(self.sem[eng], 1)
            self.cnt[eng] = n
            self.snap[(eng, n)] = dict(self.known[eng])
        self._commit((eng, n), list(reads) + list(late), list(writes) + list(late_w))
        return inst

    def dma(self, q, stream, out, in_, reads=(), writes=()):
        k = "dma%d" % (self.dma_i % self.NSLOT)
        self.dma_i += 1
        deps = self._deps(q, reads, writes)
        if self.cnt[k] > 0:
            deps.append((k, self.cnt[k]))
        self._wait(q, deps)
        inst = self.E[q].dma_start(out=out, in_=in_)
        inst.then_inc(self.sem[k], 16)
        n = self.cnt[k] + 16
        self.cnt[k] = n
        self.snap[(k, n)] = dict(self.known[q])
        self._commit((k, n), reads, writes)
        return inst

    def barrier(self):
        for eng in self.E:
            self._wait(eng, [(e, v) for e, v in self.cnt.items() if v > 0 and e != eng])


def build_program(debug=False):
    nc = bass.Bass("TRN2", target_bir_lowering=False)

    def din(name, shape, dt=F32):
        return nc.dram_tensor(name, list(shape), dt, kind="ExternalInput").ap()

    xa = din("xa", [TA, D])
    w_fl = din("w_fl", [128, 8, 8])
    bfb = din("bfb", [128, NB * 8])
    npre = din("npre", [128, 8])
    w_c = din("w_c", [128, 8, 640])
    w_fox = din("w_fox", [4, 128, 8, 512])
    w_mg = din("w_mg", [4, 128, 8, 128])
    w_uqn = din("w_uqn", [128, 2, 512])
    w_uqr = din("w_uqr", [128, 2, 512])
    w_ukk = din("w_ukk", [128, 512])
    w_ukv = din("w_ukv", [128, 512])
    w_out = din("w_out", [128, 8, 1024])
    gq = din("gq", [128, 2])
    gkv = din("gkv", [128, 1])
    gpost = din("gpost", [128, D])
    umat = din("umat", [128, 128])
    colc = din("colc", [128, 4])
    maskf = din("maskf", [128, 64], BF16)
    maskm = din("maskm", [128, 64], BF16)
    identb = din("identb", [128, 128], BF16)
    identf = din("identf", [128, 128])
    cck = din("cck", [32, TA])
    ssk = din("ssk", [32, TA])
    ccq = din("ccq", [64, NOWN])
    ssq = din("ssq", [64, NOWN])
    yout = nc.dram_tensor("yout", [NOWN, D], F32, kind="ExternalOutput").ap()
    dbg = {}

    es = ExitStack()
    with es:
        B = Builder(nc, es)
        B.init_dma_slots()

        def sb(name, shape, dt):
            return es.enter_context(nc.sbuf_tensor(name, list(shape), dt))

        def psum(name, shape, dt):
            return es.enter_context(nc.psum_tensor(name, list(shape), dt))

        hT = sb("hT", [128, 8, TA], BF16)
        attnT = sb("attnT", [128, 8, NOWN], BF16)
        KB = [sb("KB0", [128, TA], BF16), sb("KB1", [128, TA], BF16)]
        VB = sb("VB", [128, NB, 193], BF16)
        QB = [sb("QB0", [128, NOWN], BF16), sb("QB1", [128, NOWN], BF16)]
        GBt = [sb("GB0", [64, 2, 512], BF16), sb("GB1", [64, 2, 512], BF16)]
        ckvn = sb("ckvn", [128, TA], BF16)
        cqn = sb("cqn", [128, 2, NOWN], BF16)
        nbias = sb("nbias", [128, NB * 8], F32)
        c_identb = sb("c_identb", [128, 128], BF16)
        c_identf = sb("c_identf", [128, 128], F32)
        c_maskf = sb("c_maskf", [128, 64], BF16)
        c_maskm = sb("c_maskm", [128, 64], BF16)
        c_umat = sb("c_umat", [128, 128], F32)
        c_col = sb("c_col", [128, 4], F32)
        c_gq = sb("c_gq", [128, 2], F32)
        c_gkv = sb("c_gkv", [128, 1], F32)
        c_np = sb("c_np", [128, 8], F32)
        c_onesb = sb("c_onesb", [128, 128], BF16)
        c_onesf = sb("c_onesf", [128, 128], F32)
        c_nhalf = sb("c_nhalf", [128, 512], F32)
        c_bfb = sb("c_bfb", [128, NB * 8], F32)
        wst = [sb("wst0", [128, 8 * 128], F32), sb("wst1", [128, 8 * 128], F32)]
        wb = [sb("wb%d" % i, [128, 8, 128], BF16) for i in range(4)]
        wuqn = sb("wuqn", [128, 2, 128], BF16)
        wuqr = sb("wuqr", [128, 2, 128], BF16)
        wukk = sb("wukk", [128, 512], BF16)
        wukv = sb("wukv", [128, 512], BF16)
        wflb = sb("wflb", [128, 8, 8], BF16)
        f32t = [sb("f32t0", [128, 1024], F32), sb("f32t1", [128, 1024], F32),
                sb("f32t2", [128, 512], F32), sb("f32t3", [128, 512], F32)]
        bft = [sb("bft%d" % i, [128, 1024], BF16) for i in range(3)]
        Pt = [sb("Pt%d" % i, [128, 512], BF16) for i in range(3)]
        rdb = [sb("rdb0", [128, 512], BF16), sb("rdb1", [128, 512], BF16)]
        small = sb("small", [128, 64], F32)
        ssq_all = sb("ssq_all", [128, 40], F32)
        lsp = sb("lsp", [128, NB * 8], F32)
        offs = sb("offs", [128, NB * 8], F32)

        ps = [psum("ps%d" % i, [128, 512], F32) for i in range(6)]
        es_A = ExitStack()
        pt = [es_A.enter_context(nc.psum_tensor("pt%d" % i, [128, 1024], BF16)) for i in range(2)]

        R = {}

        def res(name):
            if name not in R:
                R[name] = Res(name)
            return R[name]

        r_ps = [res("ps%d" % i) for i in range(6)]
        r_pt = [res("pt%d" % i) for i in range(2)]
        r_f32t = [res("f32t%d" % i) for i in range(4)]
        r_bft = [res("bft%d" % i) for i in range(3)]
        r_P = [res("P%d" % i) for i in range(3)]
        r_wst = [res("wst0"), res("wst1")]
        r_wb = [res("wb%d" % i) for i in range(4)]
        r_KB = [res("KB0"), res("KB1")]
        r_QB = [res("QB0"), res("QB1")]
        r_GB = [res("GB0"), res("GB1")]
        r_const = res("const")
        r_rdb = [res("rdb0"), res("rdb1")]

        B.op("pool", lambda: nc.gpsimd.memset(small[:], 0.0), writes=[r_const, res("small_init")])
        B.op("pool", lambda: nc.gpsimd.memset(c_nhalf[:], -0.5), writes=[r_const, res("nhalf")])
        B.op("pool", lambda: nc.gpsimd.memset(c_onesb[:], 1.0), writes=[r_const])
        B.op("pool", lambda: nc.gpsimd.memset(c_onesf[:], 1.0), writes=[r_const])
        xbufs = [(f32t[0][:], r_f32t[0]), (f32t[1][:], r_f32t[1]), (wst[0][:], r_wst[0]), (wst[1][:], r_wst[1])]
        for c in range(8):
            xbufs.append((attnT[:, c, :].bitcast(F32), res("xstage%d" % c)))
        NPRE = len(xbufs)
        r_identb = res("identb")
        B.dma("sp", "c", c_identb[:], identb[:, :], writes=[r_identb])
        r_np = res("np")
        B.dma("sp", "c", c_np[:], npre[:, :], writes=[r_np])
        r_wfl = res("wfl")
        B.dma("sp", "w", lsp[:, 0:64].rearrange("p (k c) -> p k c", k=8), w_fl[:, :, :], writes=[res("lsp")])
        B.op("dve", lambda: nc.vector.tensor_tensor(wflb[:], lsp[:, 0:64].rearrange("p (k c) -> p k c", k=8),
                                                    c_np[:, :].unsqueeze(2).broadcast_to([128, 8, 8]), ALU.mult),
             reads=[res("lsp"), r_np], writes=[r_wfl])
        for blk in range(NPRE):
            B.dma("sp", "x", xbufs[blk][0], xa[blk * 128:(blk + 1) * 128, :], writes=[xbufs[blk][1]])

        B.op("pool", lambda: nc.gpsimd.memset(VB[:], 1.0), writes=[res("VB"), res("VBo")])

        def rsqrt_pool(out_ap, in_ap, n, scr_ap, reads, writes, scr_res):
            B.op("pool", lambda: nc.gpsimd.tensor_scalar(scr_ap, in_ap, 1.0 / n, EPS, ALU.mult, ALU.add),
                 reads=reads, writes=[scr_res])
            B.op("pool", lambda: nc.gpsimd.tensor_tensor(out_ap, scr_ap, c_nhalf[:, 0:1], ALU.pow),
                 reads=[scr_res, res("nhalf")], writes=writes)

        r_hT = res("hT")
        def stage2(blk):
            ptb, r_ptb = pt[blk % 2], r_pt[blk % 2]
            dst = hT[:, :, blk * 128:(blk + 1) * 128]
            src = ptb[:].rearrange("p (c t) -> p c t", c=8)
            B.op("dve", lambda: nc.vector.tensor_copy(dst, src), reads=[r_ptb], writes=[r_hT])

        for blk in range(NB):
            xb_ = blk % 2
            xs, r_xs = xbufs[blk % len(xbufs)]
            xn, r_xn = bft[xb_], r_bft[xb_]
            junk, r_junk = bft[2], r_bft[2]
            if blk >= NPRE:
                B.dma("sp", "x", xs, xa[blk * 128:(blk + 1) * 128, :], writes=[r_xs])
            sscol = ssq_all[:, blk:blk + 1]
            r_ss = res("ss%d" % blk)
            B.op("act", lambda: nc.scalar.activation(out=junk[:], in_=xs, func=AF.Square, accum_out=sscol),
                 reads=[r_xs], writes=[r_junk, r_ss])
            B.op("act", lambda: nc.scalar.copy(out=small[:, 62:63], in_=small[:, 63:64]), reads=[res("small_init")], writes=[r_ss, res("fence")])
            rcol = small[:, blk:blk + 1]
            r_rc = res("rc%d" % blk)
            scr = small[:, 40 + (blk % 2):41 + (blk % 2)]
            rsqrt_pool(rcol, sscol, float(D), scr, [r_ss], [r_rc], res("scrA%d" % (blk % 2)))
            B.op("dve", lambda: nc.vector.tensor_scalar(xn[:], xs, rcol, None, ALU.mult),
                 reads=[r_xs, r_rc], writes=[r_xn])
            ptb, r_ptb = pt[xb_], r_pt[xb_]
            for c in range(8):
                B.op("pe", lambda c=c: nc.tensor.transpose(out=ptb[:, c * 128:(c + 1) * 128],
                                                           in_=xn[:, c * 128:(c + 1) * 128], identity=c_identb[:]),
                     reads=[r_xn, r_identb], writes=[r_ptb], signal=(c == 7))
            if blk >= 1:
                stage2(blk - 1)
        stage2(NB - 1)
        def cload(dst, src):
            B.dma("sp", "c", dst, src, writes=[r_const])

        cload(c_identf[:], identf[:, :])
        cload(c_maskf[:], maskf[:, :])
        cload(c_maskm[:], maskm[:, :])
        cload(c_umat[:], umat[:, :])
        cload(c_col[:], colc[:, :])
        cload(c_gq[:], gq[:, :])
        cload(c_gkv[:], gkv[:, :])
        cload(c_bfb[:], bfb[:, :])

        B.barrier()
        es_A.close()
        ps.append(psum("ps6", [128, 512], F32))
        ps.append(psum("ps7", [128, 512], F32))
        r_ps.append(res("ps6"))
        r_ps.append(res("ps7"))
        for hl in range(2):
            B.op("pool", lambda hl=hl: nc.gpsimd.memset(KB[hl][64:128, :], 0.0), writes=[r_KB[hl]])
            B.op("pool", lambda hl=hl: nc.gpsimd.memset(KB[hl][64:65, :], 1.0), writes=[r_KB[hl]])
            B.op("pool", lambda hl=hl: nc.gpsimd.memset(QB[hl][64:128, :], 0.0), writes=[r_QB[hl]])

        def own_cols(k, j):
            a = (8 * j + 1) * 128
            return hT[:, k, a:a + 1024].rearrange("p (n r) -> p n r", r=128)[:, :, 0:64]

        wl_state = {"i": 0}

        def load_w8(dst_bf, r_dst, src_ap):
            i = wl_state["i"] % 2
            wl_state["i"] += 1
            B.dma("sp", "w", wst[i][:].rearrange("p (k c) -> p k c", k=8), src_ap, writes=[r_wst[i]])
            B.op("dve", lambda: nc.vector.tensor_tensor(dst_bf[:], wst[i][:].rearrange("p (k c) -> p k c", k=8),
                                                        c_np[:, :].unsqueeze(2).broadcast_to([128, 8, 128]), ALU.mult),
                 reads=[r_wst[i], r_np], writes=[r_dst])

        pl = ps[0]
        for blk in range(NB):
            for k in range(8):
                B.op("pe", lambda blk=blk, k=k: nc.tensor.matmul(pl[:, blk * 8:(blk + 1) * 8],
                                                                 lhsT=hT[:, k, blk * 128:(blk + 1) * 128],
                                                                 rhs=wflb[:, k, :], start=(k == 0), stop=(k == 7)),
                     reads=[r_hT, r_wfl], writes=[r_ps[0]], signal=(k == 7 and blk == NB - 1))
        r_lsp = res("lsp")
        r_offs = res("offs")
        NL = NB * 8
        B.op("dve", lambda: nc.vector.tensor_tensor(lsp[:], pl[:, 0:NL], c_bfb[:], ALU.add),
             reads=[r_ps[0], r_const], writes=[r_lsp])
        B.op("act", lambda: nc.scalar.activation(out=offs[:], in_=lsp[:], func=AF.Exp, scale=-1.0),
             reads=[r_lsp], writes=[r_offs])
        B.op("act", lambda: nc.scalar.activation(out=lsp[:], in_=offs[:], func=AF.Ln, bias=1.0),
             reads=[r_offs], writes=[r_lsp])
        B.op("dve", lambda: nc.vector.tensor_scalar(lsp[:, 0:8], lsp[:, 0:8], c_col[:, 1:2], None, ALU.mult),
             reads=[r_lsp, r_const], writes=[r_lsp])
        B.op("pe", lambda: nc.tensor.matmul(ps[1][:, 0:NL], lhsT=c_umat[:], rhs=lsp[:], start=True, stop=True),
             reads=[r_lsp, r_const], writes=[r_ps[1]])
        B.op("pe", lambda: nc.tensor.matmul(ps[2][:, 0:NL], lhsT=c_onesf[:], rhs=lsp[:], start=True, stop=True),
             reads=[r_lsp, r_const], writes=[r_ps[2]])
        B.op("dve", lambda: nc.vector.memset(offs[:, 0:8], 0.0), reads=[r_offs], writes=[r_offs])
        B.op("dve", lambda: nc.vector.tensor_copy(offs[:, 8:NL], ps[2][:, 0:NL - 8]), reads=[r_ps[2]], writes=[r_offs])
        sc_src, r_sc_src, sc_dst, r_sc_dst = offs, r_offs, lsp, r_lsp
        for s_ in (1, 2, 4, 8, 16, 32):
            w_ = 8 * s_
            B.op("dve", lambda a=sc_src, d=sc_dst, w_=w_: nc.vector.tensor_tensor(d[:, w_:NL], a[:, w_:NL], a[:, 0:NL - w_], ALU.add),
                 reads=[r_sc_src], writes=[r_sc_dst])
            B.op("act", lambda a=sc_src, d=sc_dst, w_=w_: nc.scalar.copy(out=d[:, 0:w_], in_=a[:, 0:w_]),
                 reads=[r_sc_src], writes=[r_sc_dst])
            sc_src, r_sc_src, sc_dst, r_sc_dst = sc_dst, r_sc_dst, sc_src, r_sc_src
        assert sc_src is offs
        r_nb = res("nbias")
        B.op("dve", lambda: nc.vector.tensor_tensor(nbias[:], ps[1][:, 0:NL], offs[:], ALU.add),
             reads=[r_ps[1], r_offs], writes=[r_nb])
        cqr = ckvn
        r_cqr = res("cqr")
        for j in range(4):
            pj, r_pj = ps[3 + (j % 2)], r_ps[3 + (j % 2)]
            for d_ in range(8):
                blk = 8 * j + 1 + d_
                B.op("pe", lambda blk=blk, d_=d_, pj=pj: nc.tensor.transpose(out=pj[0:8, d_ * 64:(d_ + 1) * 64],
                                                                             in_=nbias[0:64, blk * 8:(blk + 1) * 8],
                                                                             identity=c_identf[0:64, 0:64]),
                     reads=[r_nb, r_const], writes=[r_pj], signal=(d_ == 7))
            B.op("dve", lambda j=j, pj=pj: nc.vector.tensor_scalar(cqr[96:104, j * 512:(j + 1) * 512], pj[0:8, :], -8.0, None, ALU.mult),
                 reads=[r_pj], writes=[r_cqr])
        B.op("dve", lambda: nc.vector.tensor_scalar(nbias[:, 0:8], nbias[:, 0:8], c_col[:, 0:1], None, ALU.add),
             reads=[r_nb, r_const], writes=[r_nb])

        def mm8(out_ap, r_out, wt, r_wt, rhs_fn):
            for k in range(8):
                B.op("pe", lambda k=k: nc.tensor.matmul(out_ap, lhsT=wt[:, k, :], rhs=rhs_fn(k), start=(k == 0), stop=(k == 7)),
                     reads=[r_hT, r_wt], writes=[r_out], signal=(k == 7))

        r_VB2, r_attn = [res("VB"), res("VBo")], res("attnT")
        S_BANKS = [0, 1, 6]
        O_BANKS = [2, 3]
        gcount = {"blk": 0, "tile": 0}
        pending_fin = []
        FIN_DEFER = 7

        def attention_pair(chunk, kind, wt_gate, r_wt_gate, hook=None):
            kdim = 128 if kind == "fox" else 96
            scale = FOX_SCALE if kind == "fox" else MLA_SCALE
            cmask = c_maskf if kind == "fox" else c_maskm
            seq = []
            for j in range(4):
                for hl in range(2):
                    blocks = [(blk, 0, 512, False) for blk in range(0, 8 * j + 1)]
                    blocks += [(8 * j + 1 + d_, 64 * d_, 512 - 64 * d_, True) for d_ in range(8)]
                    tno = gcount["tile"]
                    gcount["tile"] += 1
                    for bi, (blk, c0, N, diag) in enumerate(blocks):
                        g = gcount["blk"]
                        gcount["blk"] += 1
                        seq.append(dict(j=j, hl=hl, blk=blk, c0=c0, N=N, diag=diag, first=(bi == 0), last=(bi == len(blocks) - 1),
                                        sb=S_BANKS[g % 3], pi=g % 3, ob=O_BANKS[tno % 2], t=tno))
            nseq = len(seq)

            def qk(n):
                d = seq[n]
                sp_, r_sp = ps[d["sb"]], r_ps[d["sb"]]
                Kt, r_K = KB[d["hl"]], r_KB[d["hl"]]
                Qt, r_Q = QB[d["hl"]], r_QB[d["hl"]]
                blk, c0, N, diag = d["blk"], d["c0"], d["N"], d["diag"]
                q0 = d["j"] * 512
                B.op("pe", lambda: nc.tensor.matmul(sp_[:, 0:N], lhsT=Kt[0:kdim, blk * 128:(blk + 1) * 128],
                                                    rhs=Qt[0:kdim, q0 + c0:q0 + 512], start=True, stop=(not diag)),
                     reads=[r_K], late=[r_Q], late_w=[r_sp], signal=(not diag))
                if diag:
                    B.op("pe", lambda: nc.tensor.matmul(sp_[:, 0:64], lhsT=c_identb[:], rhs=cmask[:], start=False, stop=True),
                         reads=[r_const, r_identb], writes=[r_sp], signal=True)

            def ex(n):
                d = seq[n]
                sp_, r_sp = ps[d["sb"]], r_ps[d["sb"]]
                P_, r_P_ = Pt[d["pi"]], r_P[d["pi"]]
                blk, N = d["blk"], d["N"]
                if kind == "fox":
                    col = blk * 8 + chunk * 2 + d["hl"]
                    bias = nbias[:, col:col + 1]
                    rd = [r_sp, r_nb]
                elif blk == 0:
                    bias = c_col[:, 0:1]
                    rd = [r_sp, r_const]
                else:
                    bias = 0.0
                    rd = [r_sp]
                B.op("act", lambda: nc.scalar.activation(out=P_[:, 0:N], in_=sp_[:, 0:N], func=AF.Exp, bias=bias, scale=scale),
                     reads=rd, writes=[r_P_])

            def pv(n):
                d = seq[n]
                P_, r_P_ = Pt[d["pi"]], r_P[d["pi"]]
                po, r_po = ps[d["ob"]], r_ps[d["ob"]]
                blk, c0, N, hl = d["blk"], d["c0"], d["N"], d["hl"]
                B.op("pe", lambda: nc.tensor.matmul(po[:, c0:512], lhsT=VB[:, blk, hl * 65:hl * 65 + 128], rhs=P_[:, 0:N],
                                                    start=d["first"], stop=d["last"]),
                     reads=r_VB2, late=[r_P_], late_w=[r_po], signal=d["last"])

            def fin_a(d):
                po, r_po = ps[d["ob"]], r_ps[d["ob"]]
                rden, r_rden = rdb[d["t"] % 2], r_rdb[d["t"] % 2]

                def _recip():
                    with nc.allow_low_precision(reason="softmax normaliser broadcast operand in bf16 (fp32 reciprocal, rounded once)"):
                        return nc.vector.reciprocal(rden[64:65, 0:512], po[64:65, :])
                B.op("dve", _recip, reads=[r_po], writes=[r_rden])

            def fin_b(d):
                po, r_po = ps[d["ob"]], r_ps[d["ob"]]
                pbc, r_pbc = ps[4], r_ps[4]
                hl, q0 = d["hl"], d["j"] * 512
                GBj, r_GBj = GBt[d["j"] % 2], r_GB[d["j"] % 2]
                rden, r_rden = rdb[d["t"] % 2], r_rdb[d["t"] % 2]
                tg, r_tg = f32t[3], r_f32t[3]
                B.op("pe", lambda: nc.tensor.matmul(pbc[0:64, :], lhsT=c_onesb[64:65, 0:64], rhs=rden[64:65, 0:512], start=True, stop=True),
                     reads=[r_const], late=[r_rden], late_w=[r_pbc])
                B.op("dve", lambda: nc.vector.tensor_tensor(tg[0:64, 0:512], pbc[0:64, :], GBj[:, hl, :], ALU.mult),
                     reads=[r_pbc, r_GBj], writes=[r_tg])
                B.op("dve", lambda: nc.vector.tensor_tensor(attnT[hl * 64:(hl + 1) * 64, chunk, q0:q0 + 512], po[0:64, :], tg[0:64, 0:512], ALU.mult),
                     reads=[r_po, r_tg], writes=[r_attn])

            while pending_fin:
                pending_fin.pop(0)[1]()
            gate_due = None
            gate_k = None
            gate_mm(wt_gate, r_wt_gate, 0)
            gate_ep(0, GBt[0], r_GB[0])
            qk(0)
            if nseq > 1:
                qk(1)
            for n in range(nseq):
                d = seq[n]
                if d["first"] and d["hl"] == 0:
                    if d["j"] > 0:
                        gate_k = [d["j"], 0]
                        gate_due = (n + 12, d["j"])
                if gate_k is not None:
                    jj, kk = gate_k
                    B.op("pe", lambda jj=jj, kk=kk: nc.tensor.matmul(ps[5][:], lhsT=wt_gate[:, kk, :], rhs=own_cols(kk, jj),
                                                                     start=(kk == 0), stop=(kk == 7)),
                         reads=[r_hT, r_wt_gate], writes=[r_ps[5]], signal=(kk == 7))
                    gate_k = [jj, kk + 1] if kk < 7 else None
                    if kk == 7 and jj == 3 and hook is not None:
                        hook()
                if gate_due is not None and n >= gate_due[0]:
                    jj = gate_due[1]
                    gate_ep(jj, GBt[jj % 2], r_GB[jj % 2])
                    gate_due = None
                if n + 2 < nseq:
                    qk(n + 2)
                ex(n)
                pv(n)
                if pending_fin and n >= pending_fin[0][0] + FIN_DEFER:
                    pending_fin.pop(0)[1]()
                if d["last"]:
                    fin_a(d)
                    pending_fin.append((n, lambda d=d: fin_b(d)))
            for i in range(len(pending_fin)):
                pending_fin[i] = (-10**9, pending_fin[i][1])

        def gate_mm(wt, r_wt, j):
            pg, r_pg = ps[5], r_ps[5]
            mm8(pg[:], r_pg, wt, r_wt, lambda k: own_cols(k, j))

        def gate_ep(j, GBj, r_GBj):
            pg, r_pg = ps[5], r_ps[5]
            e_, r_e = f32t[j % 2], r_f32t[j % 2]
            B.op("act", lambda: nc.scalar.activation(out=e_[:, 0:512], in_=pg[:], func=AF.Exp, scale=-1.0), reads=[r_pg], writes=[r_e])
            B.op("dve", lambda: nc.vector.tensor_scalar(e_[:, 0:512], e_[:, 0:512], 1.0, None, ALU.add), reads=[r_e], writes=[r_e])
            B.op("dve", lambda: nc.vector.reciprocal(e_[:, 512:1024], e_[:, 0:512]), reads=[r_e], writes=[r_e])
            for hl in range(2):
                B.op("dve", lambda hl=hl: nc.vector.tensor_tensor(GBj[:, hl, :], pg[hl * 64:(hl + 1) * 64, :],
                                                                  e_[hl * 64:(hl + 1) * 64, 512:1024], ALU.mult),
                     reads=[r_pg, r_e], writes=[r_GBj])

        def evac_split(pa, r_pa, W, dst_fn, r_dsts, i):
            for hl in range(2):
                if hl == 0:
                    B.op("act", lambda hl=hl: nc.scalar.copy(out=dst_fn(hl), in_=pa[hl * 64:(hl + 1) * 64, 0:W]), reads=[r_pa], writes=[r_dsts[hl]])
                else:
                    B.op("dve", lambda hl=hl: nc.vector.tensor_copy(dst_fn(hl), pa[hl * 64:(hl + 1) * 64, 0:W]), reads=[r_pa], writes=[r_dsts[hl]])

        PB = [5, 7, 0, 1, 6]
        pb_state = {"i": 0}

        def next_pb():
            i = PB[pb_state["i"] % len(PB)]
            pb_state["i"] += 1
            return ps[i], r_ps[i]

        def v_proj(mm_fn):
            vi = 0
            for b0 in range(0, NB, 4):
                nb_ = min(4, NB - b0)
                pv_, r_pv = next_pb()
                for q in range(nb_):
                    mm_fn(pv_[:, q * 128:(q + 1) * 128], b0 + q, r_pv, q == nb_ - 1)
                dstv = VB[:, b0:b0 + nb_, 0:130].rearrange("p b (h c) -> p b h c", c=65)[:, :, :, 0:64]
                srcv = pv_[:, 0:nb_ * 128].rearrange("p (b h d) -> p b h d", b=nb_, h=2)
                if vi % 2 == 0:
                    B.op("act", lambda: nc.scalar.copy(out=dstv, in_=srcv), reads=[r_pv], writes=[r_VB2[0]])
                else:
                    B.op("dve", lambda: nc.vector.tensor_copy(dstv, srcv), reads=[r_pv], writes=[r_VB2[1]])
                vi += 1

        gstate = {"i": 0}

        def fox_weights(hp):
            for i in range(4):
                load_w8(wb[i], r_wb[i], w_fox[hp, :, :, i * 128:(i + 1) * 128])

        def phase_c_weights():
            for i in range(3):
                load_w8(wb[i], r_wb[i], w_c[:, :, (2 + i) * 128:(3 + i) * 128])

        fox_weights(0)
        for hp in range(4):
            for ti, (t0, W) in enumerate(TILES):
                pk, r_pk = next_pb()
                mm8(pk[:, 0:W], r_pk, wb[0], r_wb[0], lambda k: hT[:, k, t0:t0 + W])
                evac_split(pk, r_pk, W, lambda hl: KB[hl][0:64, t0:t0 + W], r_KB, ti)
            def fox_v_mm(out_ap, blk, r_bank, last):
                for k in range(8):
                    B.op("pe", lambda k=k: nc.tensor.matmul(out_ap, lhsT=hT[:, k, blk * 128:(blk + 1) * 128], rhs=wb[3][:, k, :],
                                                            start=(k == 0), stop=(k == 7)),
                         reads=[r_hT, r_wb[3]], writes=[r_bank], signal=(k == 7 and last))
            v_proj(fox_v_mm)
            for j in range(4):
                pq_, r_pq = next_pb()
                mm8(pq_[:], r_pq, wb[1], r_wb[1], lambda k: own_cols(k, j))
                evac_split(pq_, r_pq, 512, lambda hl: QB[hl][0:64, j * 512:(j + 1) * 512], r_QB, j)
            for hl in range(2):
                h = 2 * hp + hl
                B.dma("sp", "mv", QB[hl][64:65, :], cqr[96 + h:97 + h, 0:NOWN], reads=[r_cqr], writes=[r_QB[hl]])
            attention_pair(hp, "fox", wb[2], r_wb[2], hook=((lambda hp=hp: fox_weights(hp + 1)) if hp < 3 else phase_c_weights))

        r_ckvn, r_cqn = res("ckvn"), res("cqn")

        def norm_p1(psrc_list, W, par=0):
            single = (len(psrc_list) == 1)
            tmps = []
            for c, (pa, r_pa) in enumerate(psrc_list):
                if single:
                    tmp_ap, r_tmp = f32t[par][:, 0:W], r_f32t[par]
                    sq_ap, r_sq = bft[par][:, 0:W], r_bft[par]
                else:
                    tmp_ap, r_tmp = f32t[c][:, par * 512:par * 512 + W], r_f32t[c]
                    sq_ap, r_sq = bft[c][:, par * 512:par * 512 + W], r_bft[c]
                B.op("act", lambda pa=pa, tmp_ap=tmp_ap: nc.scalar.copy(out=tmp_ap, in_=pa), reads=[r_pa], writes=[r_tmp])
                B.op("dve", lambda pa=pa, tmp_ap=tmp_ap, sq_ap=sq_ap: nc.vector.tensor_tensor(sq_ap, pa, tmp_ap, ALU.mult),
                     reads=[r_pa, r_tmp], writes=[r_sq])
                tmps.append((tmp_ap, r_tmp, sq_ap, r_sq))
            return tmps

        def norm_p2(tmps, n_feat, gcols, dst_fn, r_dst, W, par=0):
            pq, r_pq = ps[4], r_ps[4]
            for c, (tmp_ap, r_tmp, sq_ap, r_sq) in enumerate(tmps):
                B.op("pe", lambda c=c, sq_ap=sq_ap: nc.tensor.matmul(pq[:, 0:W], lhsT=c_onesb[:], rhs=sq_ap,
                                                                     start=(c == 0), stop=(c == len(tmps) - 1)),
                     reads=[r_sq, r_const], writes=[r_pq], signal=(c == len(tmps) - 1))
            rr, r_rr = f32t[2 + par], r_f32t[2 + par]
            B.op("act", lambda: nc.scalar.activation(out=rr[:, 0:W], in_=pq[:, 0:W], func=AF.Ln, scale=1.0 / n_feat, bias=EPS),
                 reads=[r_pq], writes=[r_rr])
            B.op("act", lambda: nc.scalar.activation(out=rr[:, 0:W], in_=rr[:, 0:W], func=AF.Exp, scale=-0.5),
                 reads=[r_rr], writes=[r_rr])
            for c, (tmp_ap, r_tmp, sq_ap, r_sq) in enumerate(tmps):
                B.op("dve", lambda c=c, tmp_ap=tmp_ap: nc.vector.scalar_tensor_tensor(dst_fn(c), tmp_ap, gcols[:, c:c + 1], rr[:, 0:W],
                                                                                      ALU.mult, ALU.mult),
                     reads=[r_tmp, r_rr, r_const], writes=[r_dst])

        while pending_fin:
            pending_fin.pop(0)[1]()
        B.barrier()
        r_half = [[res("wst%d_h%d" % (i, h)) for h in range(2)] for i in range(2)]
        for ti, (t0, W) in enumerate(TILES):
            par = ti % 2
            b0, b1, b2 = (0, 1, 2) if par == 0 else (5, 6, 7)
            mm8(ps[b0][:, 0:W], r_ps[b0], wb[0], r_wb[0], lambda k: hT[:, k, t0:t0 + W])
            st_kv = norm_p1([(ps[b0][:, 0:W], r_ps[b0])], W, par)
            mm8(ps[b1][:, 0:W], r_ps[b1], wb[1], r_wb[1], lambda k: hT[:, k, t0:t0 + W])
            mm8(ps[b2][:, 0:W], r_ps[b2], wb[2], r_wb[2], lambda k: hT[:, k, t0:t0 + W])
            norm_p2(st_kv, 128.0, c_gkv, lambda c: ckvn[:, t0:t0 + W], r_ckvn, W, par)
            ct, r_ct = wst[0][:, par * 512:(par + 1) * 512], r_half[0][par]
            stt, r_stt = wst[1][:, par * 512:(par + 1) * 512], r_half[1][par]
            B.dma("sp", "tab", ct[64:96, 0:W], cck[:, t0:t0 + W], writes=[r_ct])
            B.dma("sp", "tab", stt[64:96, 0:W], ssk[:, t0:t0 + W], writes=[r_stt])
            B.op("dve", lambda: nc.vector.tensor_tensor(ct[64:96, 0:W], ps[b1][64:96, 0:W], ct[64:96, 0:W], ALU.mult),
                 reads=[r_ps[b1], r_ct], writes=[r_ct])
            B.op("dve", lambda: nc.vector.tensor_tensor(stt[64:96, 0:W], ps[b2][64:96, 0:W], stt[64:96, 0:W], ALU.mult),
                 reads=[r_ps[b2], r_stt], writes=[r_stt])
            for hl in range(2):
                B.op("dve", lambda hl=hl: nc.vector.tensor_tensor(KB[hl][64:96, t0:t0 + W], ct[64:96, 0:W], stt[64:96, 0:W], ALU.add),
                     reads=[r_ct, r_stt], writes=[r_KB[hl]])
        B.barrier()
        for i in range(2):
            load_w8(wb[i], r_wb[i], w_c[:, :, i * 128:(i + 1) * 128])
        r_wu = res("wu")
        r_wuq = res("wuq")
        for (dstw, srcw) in ((wukk, w_ukk), (wukv, w_ukv)):
            st, r_st = f32t[0], r_f32t[0]
            B.dma("sp", "w", st[:, 0:512], srcw[:, :], writes=[r_st])
            B.op("dve", lambda dstw=dstw, st=st: nc.vector.tensor_copy(dstw[:], st[:, 0:512]), reads=[r_st], writes=[r_wu])

        def mla_weights(hp):
            load_w8(wb[2], r_wb[2], w_mg[hp, :, :, :])
            for (dstw, srcw) in ((wuqn, w_uqn), (wuqr, w_uqr)):
                st, r_st = wst[1], r_wst[1]
                B.dma("sp", "w", st[:, 0:256].rearrange("p (c n) -> p c n", c=2), srcw[:, :, hp * 128:(hp + 1) * 128], writes=[r_st])
                B.op("dve", lambda dstw=dstw, st=st: nc.vector.tensor_copy(dstw[:].rearrange("p c n -> p (c n)"), st[:, 0:256]),
                     reads=[r_st], writes=[r_wuq])

        prev_q = None
        for j in range(4):
            par = j % 2
            b0, b1 = (0, 1) if par == 0 else (5, 6)
            mm8(ps[b0][:], r_ps[b0], wb[0], r_wb[0], lambda k: own_cols(k, j))
            mm8(ps[b1][:], r_ps[b1], wb[1], r_wb[1], lambda k: own_cols(k, j))
            st_q = norm_p1([(ps[b0][:], r_ps[b0]), (ps[b1][:], r_ps[b1])], 512, par)
            if prev_q is not None:
                norm_p2(prev_q[0], 256.0, c_gq, lambda c, jj=prev_q[1]: cqn[:, c, jj * 512:(jj + 1) * 512], r_cqn, 512, prev_q[2])
            prev_q = (st_q, j, par)
        norm_p2(prev_q[0], 256.0, c_gq, lambda c, jj=prev_q[1]: cqn[:, c, jj * 512:(jj + 1) * 512], r_cqn, 512, prev_q[2])

        r_wo = res("wo")

        def wo_view(k):
            if k < 4:
                return ckvn[:, k * 1024:(k + 1) * 1024]
            return cqn[:, (k - 4) // 2, ((k - 4) % 2) * 1024:((k - 4) % 2 + 1) * 1024]

        NXO = 8
        xo_slots = [(hT[:, k, 0:2048].bitcast(F32), res("xo%d" % k)) for k in range(NXO)]
        yo_slots = [(hT[:, k, 2048:4096].bitcast(F32), res("yo%d" % k)) for k in range(NXO)]
        o_state = {"hT_fenced": False}

        def load_xo(ob):
            xo, r_xo = xo_slots[ob % NXO]
            n0 = 1 + 2 * ob
            wr = [r_xo] + ([r_hT] if ob < NXO else [])
            B.dma("sp", "x", xo[0:64, :], xa[n0 * 128:n0 * 128 + 64, :], writes=wr)
            B.dma("sp", "x", xo[64:128, :], xa[(n0 + 1) * 128:(n0 + 1) * 128 + 64, :], writes=[r_xo])

        def phase_o_prefetch():
            for ob in range(NXO):
                load_xo(ob)

        mla_weights(0)
        for hp in range(4):
            for ti, (t0, W) in enumerate(TILES):
                pk, r_pk = next_pb()
                B.op("pe", lambda: nc.tensor.matmul(pk[:, 0:W], lhsT=wukk[:, hp * 128:(hp + 1) * 128], rhs=ckvn[:, t0:t0 + W], start=True, stop=True),
                     reads=[r_wu, r_ckvn], writes=[r_pk])
                evac_split(pk, r_pk, W, lambda hl: KB[hl][0:64, t0:t0 + W], r_KB, ti)
            def mla_v_mm(out_ap, blk, r_bank, last):
                B.op("pe", lambda: nc.tensor.matmul(out_ap, lhsT=ckvn[:, blk * 128:(blk + 1) * 128], rhs=wukv[:, hp * 128:(hp + 1) * 128],
                                                    start=True, stop=True),
                     reads=[r_wu, r_ckvn], writes=[r_bank], signal=last)
            v_proj(mla_v_mm)
            qsubs = [res("qrope_%s_%d" % (nm, p_)) for nm in ("t1", "t2", "t2b") for p_ in range(2)]
            B.op("dve", lambda: nc.vector.memset(small[:, 60:61], 0.0), writes=[r_f32t[0], r_f32t[1], res("qfence")] + qsubs)
            for j in range(4):
                qs = slice(j * 512, (j + 1) * 512)
                pn, r_pn = next_pb()
                pr, r_pr = next_pb()
                par = j % 2
                c0, c1 = par * 512, (1 - par) * 512
                r_t1, r_t2, r_t2b = res("qrope_t1_%d" % par), res("qrope_t2_%d" % par), res("qrope_t2b_%d" % par)
                t1 = f32t[0][0:64, c0:c0 + 512]
                t2 = f32t[1][64:128, c0:c0 + 512]
                t2b = f32t[1][0:64, c1:c1 + 512]
                B.dma("sp", "tab", t1, ccq[:, qs], writes=[r_t1])
                B.dma("sp", "tab", t2, ssq[:, qs], writes=[r_t2])
                for (pp, r_pp, wq) in ((pn, r_pn, wuqn), (pr, r_pr, wuqr)):
                    for c in range(2):
                        B.op("pe", lambda c=c, pp=pp, wq=wq: nc.tensor.matmul(pp[:], lhsT=wq[:, c, :], rhs=cqn[:, c, qs],
                                                                              start=(c == 0), stop=(c == 1)),
                             reads=[r_wuq, r_cqn], writes=[r_pp], signal=(c == 1))
                for hl in range(2):
                    B.op("act", lambda hl=hl: nc.scalar.copy(out=QB[hl][0:64, qs], in_=pn[hl * 64:(hl + 1) * 64, :]), reads=[r_pn], writes=[r_QB[hl]])
                B.op("dve", lambda: nc.vector.tensor_tensor(t1, pr[0:64, :], t1, ALU.mult),
                     reads=[r_pr, r_t1], writes=[r_t1])
                B.op("dve", lambda: nc.vector.tensor_tensor(t2b, pr[64:128, :], t2, ALU.mult),
                     reads=[r_pr, r_t2], writes=[r_t2b])
                for hl in range(2):
                    B.op("dve", lambda hl=hl: nc.vector.tensor_tensor(QB[hl][64:96, qs], f32t[0][hl * 32:(hl + 1) * 32, c0:c0 + 512],
                                                                      f32t[1][hl * 32:(hl + 1) * 32, c1:c1 + 512], ALU.add),
                         reads=[r_t1, r_t2b], writes=[r_QB[hl]])
            B.op("dve", lambda: nc.vector.memset(small[:, 60:61], 0.0), writes=[r_f32t[0], r_f32t[1], res("qfence")] + qsubs)
            if hp == 3:
                for k in range(8):
                    i = k % 2
                    B.dma("sp", "w", wst[i][:], w_out[:, k, :], writes=[r_wst[i]])
                    B.op("dve", lambda k=k, i=i: nc.vector.tensor_copy(wo_view(k), wst[i][:]), reads=[r_wst[i]], writes=[r_wo, r_ckvn if k < 4 else r_cqn])
            attention_pair(4 + hp, "mla", wb[2], r_wb[2], hook=((lambda hp=hp: mla_weights(hp + 1)) if hp < 3 else phase_o_prefetch))

        while pending_fin:
            pending_fin.pop(0)[1]()
        gp, r_gp = wst[0], r_wst[0]
        B.dma("sp", "c", gp[:], gpost[:, :], writes=[r_gp])
        for ob in range(16):
            q4 = ob % 4
            pyA, r_pyA = ps[q4 * 2], r_ps[q4 * 2]
            pyB, r_pyB = ps[q4 * 2 + 1], r_ps[q4 * 2 + 1]
            for (py, r_py, c0) in ((pyA, r_pyA, 0), (pyB, r_pyB, 512)):
                for c in range(8):
                    B.op("pe", lambda c=c, py=py, c0=c0: nc.tensor.matmul(py[:], lhsT=attnT[:, c, ob * 128:(ob + 1) * 128], rhs=wo_view(c)[:, c0:c0 + 512],
                                                                          start=(c == 0), stop=(c == 7)),
                         reads=[r_attn, r_wo], writes=[r_py], signal=(c == 7))
            junk, r_junk = bft[ob % 2], r_bft[ob % 2]
            o4 = 4 * q4
            ssA, ssB, rO, scrO = (small[:, o4 + q:o4 + q + 1] for q in range(4))
            r_s = res("sO%d" % q4)
            B.op("act", lambda: nc.scalar.activation(out=junk[:, 0:512], in_=pyA[:], func=AF.Square, accum_out=ssA),
                 reads=[r_pyA], writes=[r_junk, r_s])
            B.op("act", lambda: nc.scalar.activation(out=junk[:, 512:1024], in_=pyB[:], func=AF.Square, accum_out=ssB),
                 reads=[r_pyB], writes=[r_junk, r_s])
            B.op("act", lambda: nc.scalar.copy(out=small[:, 62:63], in_=small[:, 63:64]), reads=[res("small_init")], writes=[r_s, res("fence")])
            B.op("pool", lambda: nc.gpsimd.tensor_tensor(ssA, ssA, ssB, ALU.add), reads=[r_s], writes=[r_s])
            rsqrt_pool(rO, ssA, float(D), scrO, [r_s], [r_s], r_s)
            xo, r_xo = xo_slots[ob % NXO]
            yo, r_yo = yo_slots[ob % NXO]
            wy = [r_yo] + ([r_hT] if ob < NXO else [])
            B.op("dve", lambda: nc.vector.scalar_tensor_tensor(yo[:, 0:512], pyA[:], rO, gp[:, 0:512], ALU.mult, ALU.mult),
                 reads=[r_pyA, r_s, r_gp], writes=wy)
            B.op("dve", lambda: nc.vector.scalar_tensor_tensor(yo[:, 512:1024], pyB[:], rO, gp[:, 512:1024], ALU.mult, ALU.mult),
                 reads=[r_pyB, r_s, r_gp], writes=[r_yo])
            B.op("dve", lambda: nc.vector.tensor_tensor(yo, yo, xo, ALU.add), reads=[r_xo, r_yo], writes=[r_yo])
            if ob + NXO < 16:
                load_xo(ob + NXO)
            B.dma("sp", "out", yout[ob * 128:(ob + 1) * 128, :], yo, reads=[r_yo])
        B._wait("sp", [(k, v) for k, v in B.cnt.items() if k.startswith("dma") and v > 0])
        print("build: counts", B.cnt, "waits", B.nwaits, "sbuf left", nc.sbuf_bytes_remaining)
    return nc


def _bf16(a):
    return np.ascontiguousarray(a.astype(ml_dtypes.bfloat16))


def _rope_tables(pos):
    inv = (10000.0 ** (-np.arange(0, 32, 2, dtype=np.float32) / np.float32(32))).astype(np.float32)
    ang = pos.astype(np.float32)[:, None] * inv[None, :]
    c, s = np.cos(ang).astype(np.float32), np.sin(ang).astype(np.float32)
    cc = np.concatenate([c, c], axis=1).T
    ss = np.concatenate([-s, s], axis=1).T
    return np.ascontiguousarray(cc), np.ascontiguousarray(ss)


def _k8(w):
    return np.ascontiguousarray(w.reshape(8, 128, -1).transpose(1, 0, 2))


def make_inputs(x, meta, norm_pre, norm_post, w_in, b_f, q_norm, w_uq, kv_norm, w_ukv, w_out):
    x = np.asarray(x, np.float32)
    w_in = np.asarray(w_in, np.float32)[0]
    w_uq = np.asarray(w_uq, np.float32)[0]
    w_ukv = np.asarray(w_ukv, np.float32)[0]
    w_out = np.asarray(w_out, np.float32)[0]
    common = {}
    common["w_fl"] = _k8(w_in[:, O_FL:O_FL + 8])
    common["bfb"] = np.ascontiguousarray(np.tile(np.asarray(b_f, np.float32)[0][None, :], (128, NB)))
    npre = np.asarray(norm_pre, np.float32)[0].reshape(8, 128).T
    common["npre"] = np.ascontiguousarray(npre)
    z64 = np.zeros((D, 64), np.float32)
    z32 = np.zeros((D, 32), np.float32)
    kr = w_in[:, O_KR:O_KR + 32]
    krs = np.concatenate([kr[:, 16:32], kr[:, 0:16]], axis=1)
    wc = np.concatenate([w_in[:, O_CQ:O_CQ + 256], w_in[:, O_CKV:O_CKV + 128], z64, kr, z32, z64, krs, z32], axis=1)
    common["w_c"] = _k8(wc)
    wf = []
    for hp in range(4):
        sl = slice(hp * 128, (hp + 1) * 128)
        wf.append(_k8(np.concatenate([w_in[:, O_FK:O_FK + 512][:, sl], w_in[:, O_FQ:O_FQ + 512][:, sl],
                                      w_in[:, O_FG:O_FG + 512][:, sl], w_in[:, O_FV:O_FV + 512][:, sl]], axis=1)))
    common["w_fox"] = np.ascontiguousarray(np.stack(wf))
    common["w_mg"] = np.ascontiguousarray(np.stack([_k8(w_in[:, O_MG + hp * 128:O_MG + (hp + 1) * 128]) for hp in range(4)]))
    wq = w_uq.reshape(256, 8, 96)

    def _k2(w):
        return np.ascontiguousarray(w.reshape(2, 128, -1).transpose(1, 0, 2))
    common["w_uqn"] = _k2(wq[:, :, 0:64].reshape(256, 512))
    ra = wq[:, :, 64:96].reshape(256, 4, 64)
    rb = np.concatenate([wq[:, :, 80:96], wq[:, :, 64:80]], axis=2).reshape(256, 4, 64)
    common["w_uqr"] = _k2(np.concatenate([ra, rb], axis=2).reshape(256, 512))
    wkv = w_ukv.reshape(128, 8, 128)
    common["w_ukk"] = np.ascontiguousarray(wkv[:, :, 0:64].reshape(128, 512))
    common["w_ukv"] = np.ascontiguousarray(wkv[:, :, 64:128].reshape(128, 512))
    common["w_out"] = _k8(w_out)
    common["gq"] = np.ascontiguousarray(np.asarray(q_norm, np.float32)[0].reshape(2, 128).T)
    common["gkv"] = np.ascontiguousarray(np.asarray(kv_norm, np.float32)[0].reshape(128, 1))
    common["gpost"] = np.ascontiguousarray(np.tile(np.asarray(norm_post, np.float32)[0][None, :], (128, 1)))
    colc = np.zeros((128, 4), np.float32)
    colc[16:, 0] = NEG
    colc[:16, 1] = 1.0
    common["colc"] = colc
    common["identb"] = _bf16(np.eye(128, dtype=np.float32))
    common["identf"] = np.eye(128, dtype=np.float32)
    meta = np.asarray(meta, np.float32)
    in_maps = []
    r = np.arange(128)
    for core in range(8):
        b, g = core // 2, core % 2
        m = dict(common)
        xa = np.zeros((NB, 128, D), np.float32)
        xa[0, :16] = meta
        xc = x[b].reshape(32, 2, 64, D)
        xa[1:, 0:64] = xc[:, g]
        xa[1:, 64:128] = xc[:, 1 - g]
        m["xa"] = xa.reshape(TA, D)
        tpos = (r + 64 * g) % 128 if g == 1 else r
        if g == 1:
            tpos = np.where(r < 64, r + 64, r - 64)
        m["umat"] = np.ascontiguousarray((tpos[:, None] <= tpos[None, :]).astype(np.float32))
        mf = np.full((128, 64), NEG, np.float32)
        mm_ = np.full((128, 64), NEG, np.float32)
        cq_ = np.arange(64)
        own_ok = (r[:64, None] <= cq_[None, :])
        mf[:64] = np.where(own_ok, 0.0, NEG)
        mm_[:64] = 0.0
        if g == 1:
            mf[64:] = 0.0
            mm_[64:] = 0.0
        m["maskf"] = _bf16(mf)
        m["maskm"] = _bf16(mm_)
        pos = np.zeros((NB, 128), np.float32)
        pos[0] = np.arange(128)
        nn = np.arange(32)[:, None]
        rr = np.arange(128)[None, :]
        chunk = np.where(rr < 64, 2 * nn + g, 2 * nn + 1 - g)
        pos[1:] = 16 + 64 * chunk + (rr % 64)
        cc, ss = _rope_tables(pos.reshape(-1))
        m["cck"], m["ssk"] = cc, ss
        posq = pos[1:, 0:64].reshape(-1)
        cc, ss = _rope_tables(posq)
        m["ccq"] = np.ascontiguousarray(np.concatenate([cc, cc], axis=0))
        m["ssq"] = np.ascontiguousarray(np.concatenate([ss, ss], axis=0))
        in_maps.append(m)
    return in_maps


_CACHE = {}


def kernel(x, meta, norm_pre, norm_post, w_in, b_f, q_norm, w_uq, kv_norm, w_ukv, w_out):
    if "nc" not in _CACHE:
        _CACHE["nc"] = build_program()
    nc = _CACHE["nc"]
    in_maps = make_inputs(x, meta, norm_pre, norm_post, w_in, b_f, q_norm, w_uq, kv_norm, w_ukv, w_out)
    res = run_bass_kernel_spmd(nc, in_maps, core_ids=list(range(8)))
    out = np.zeros((4, 32, 2, 64, D), np.float32)
    for core in range(8):
        b, g = core // 2, core % 2
        out[b, :, g] = np.asarray(res.results[core]["yout"], np.float32).reshape(32, 64, D)
    return out.reshape(4, S, D)
```

```python
import os
import numpy as np
import ml_dtypes
from contextlib import ExitStack
import concourse.bass as bass
import concourse.mybir as mybir
from concourse.bass_utils import run_bass_kernel_spmd

F32 = mybir.dt.float32
BF16 = mybir.dt.bfloat16
AF = mybir.ActivationFunctionType
ALU = mybir.AluOpType

D = 1024
S = 4096
NB = 33
TA = NB * 128
NOWN = 2048
EPS = 1e-6
NEG = -30000.0
FOX_SCALE = 64 ** -0.5
MLA_SCALE = 96 ** -0.5
TILES = [(t * 512, min(512, TA - t * 512)) for t in range((TA + 511) // 512)]

O_FQ, O_FK, O_FV, O_FL, O_FG, O_CQ, O_CKV, O_KR, O_MG = 0, 512, 1024, 1536, 1544, 2056, 2312, 2440, 2472


class Res:
    __slots__ = ("name", "lw", "rd")

    def __init__(self, name):
        self.name = name
        self.lw = None
        self.rd = {}


class Builder:
    def __init__(self, nc, es):
        self.nc = nc
        self.E = {"pe": nc.tensor, "act": nc.scalar, "dve": nc.vector, "pool": nc.gpsimd, "sp": nc.sync}
        self.sem = {}
        self.cnt = {}
        for e in ("pe", "act", "dve", "pool"):
            self.sem[e] = es.enter_context(nc.semaphore("s_" + e))
            self.cnt[e] = 0
        self.es = es
        self.known = {e: {} for e in self.E}
        self.snap = {}
        self.nwaits = 0

    NSLOT = 40

    def stream(self, name):
        pass

    def init_dma_slots(self):
        self.dma_i = 0
        for i in range(self.NSLOT):
            k = "dma%d" % i
            self.sem[k] = self.es.enter_context(self.nc.semaphore("d_%d" % i))
            self.cnt[k] = 0

    def _wait(self, eng, deps):
        kn = self.known[eng]
        best = {}
        for (e, v) in deps:
            if v > best.get(e, 0):
                best[e] = v
        for e, v in best.items():
            if kn.get(e, 0) >= v:
                continue
            assert v <= self.cnt[e], (eng, e, v, self.cnt[e])
            self.E[eng].wait_ge(self.sem[e], v)
            self.nwaits += 1
            kn[e] = v
            sn = self.snap.get((e, v))
            if sn:
                for k2, v2 in sn.items():
                    if kn.get(k2, 0) < v2:
                        kn[k2] = v2

    def _deps(self, eng, reads, writes):
        deps = []
        for r in reads:
            if r.lw is not None:
                if not (r.lw[0] == eng and eng == "pe"):
                    deps.append(r.lw)
        for w in writes:
            if w.lw is not None and not (w.lw[0] == eng and eng == "pe"):
                deps.append(w.lw)
            for e, v in w.rd.items():
                if not (e == eng and eng == "pe"):
                    deps.append((e, v))
        return deps

    def _commit(self, ev, reads, writes):
        for w in writes:
            w.lw = ev
            w.rd = {}
        for r in reads:
            if r.rd.get(ev[0], 0) < ev[1]:
                r.rd[ev[0]] = ev[1]

    def op(self, eng, fn, reads=(), writes=(), signal=True, late=(), late_w=()):
        self._wait(eng, self._deps(eng, reads, writes))
        ldeps = self._deps(eng, late, late_w)
        kn = self.known[eng]
        best = {}
        for (e, v) in ldeps:
            if v > best.get(e, 0) and kn.get(e, 0) < v:
                best[e] = v
        attach = None
        if len(best) == 1:
            attach = list(best.items())[0]
        elif len(best) > 1:
            self._wait(eng, ldeps)
        inst = fn()
        if attach is not None:
            e, v = attach
            assert v <= self.cnt[e], (eng, e, v, self.cnt[e])
            inst._wait_ge(self.sem[e], v)
            self.nwaits += 1
            kn[e] = v
            sn = self.snap.get((e, v))
            if sn:
                for k2, v2 in sn.items():
                    if kn.get(k2, 0) < v2:
                        kn[k2] = v2
        n = self.cnt[eng] + 1
        if signal:
            inst.then_inc(self.sem[eng], 1)
            self.cnt[eng] = n
            self.snap[(eng, n)] = dict(self.known[eng])
        self._commit((eng, n), list(reads) + list(late), list(writes) + list(late_w))
        return inst

    def dma(self, q, stream, out, in_, reads=(), writes=()):
        k = "dma%d" % (self.dma_i % self.NSLOT)
        self.dma_i += 1
        deps = self._deps(q, reads, writes)
        if self.cnt[k] > 0:
            deps.append((k, self.cnt[k]))
        self._wait(q, deps)
        inst = self.E[q].dma_start(out=out, in_=in_)
        inst.then_inc(self.sem[k], 16)
        n = self.cnt[k] + 16
        self.cnt[k] = n
        self.snap[(k, n)] = dict(self.known[q])
        self._commit((k, n), reads, writes)
        return inst

    def barrier(self):
        for eng in self.E:
            self._wait(eng, [(e, v) for e, v in self.cnt.items() if v > 0 and e != eng])


def build_program(debug=False):
    nc = bass.Bass("TRN2", target_bir_lowering=False)

    def din(name, shape, dt=F32):
        return nc.dram_tensor(name, list(shape), dt, kind="ExternalInput").ap()

    xa = din("xa", [TA, D])
    w_fl = din("w_fl", [128, 8, 8])
    bfb = din("bfb", [128, NB * 8])
    npre = din("npre", [128, 8])
    w_c = din("w_c", [128, 8, 640])
    w_fox = din("w_fox", [4, 128, 8, 512])
    w_mg = din("w_mg", [4, 128, 8, 128])
    w_uqn = din("w_uqn", [128, 2, 512])
    w_uqr = din("w_uqr", [128, 2, 512])
    w_ukk = din("w_ukk", [128, 512])
    w_ukv = din("w_ukv", [128, 512])
    w_out = din("w_out", [128, 8, 1024])
    gq = din("gq", [128, 2])
    gkv = din("gkv", [128, 1])
    gpost = din("gpost", [128, D])
    umat = din("umat", [128, 128])
    colc = din("colc", [128, 4])
    maskf = din("maskf", [128, 64], BF16)
    maskm = din("maskm", [128, 64], BF16)
    identb = din("identb", [128, 128], BF16)
    identf = din("identf", [128, 128])
    cck = din("cck", [32, TA])
    ssk = din("ssk", [32, TA])
    ccq = din("ccq", [64, NOWN])
    ssq = din("ssq", [64, NOWN])
    yout = nc.dram_tensor("yout", [NOWN, D], F32, kind="ExternalOutput").ap()
    dbg = {}

    es = ExitStack()
    with es:
        B = Builder(nc, es)
        B.init_dma_slots()

        def sb(name, shape, dt):
            return es.enter_context(nc.sbuf_tensor(name, list(shape), dt))

        def psum(name, shape, dt):
            return es.enter_context(nc.psum_tensor(name, list(shape), dt))

        hT = sb("hT", [128, 8, TA], BF16)
        attnT = sb("attnT", [128, 8, NOWN], BF16)
        KB = [sb("KB0", [128, TA], BF16), sb("KB1", [128, TA], BF16)]
        VB = sb("VB", [128, NB, 193], BF16)
        QB = [sb("QB0", [128, NOWN], BF16), sb("QB1", [128, NOWN], BF16)]
        GBt = [sb("GB0", [64, 2, 512], BF16), sb("GB1", [64, 2, 512], BF16)]
        ckvn = sb("ckvn", [128, TA], BF16)
        cqn = sb("cqn", [128, 2, NOWN], BF16)
        nbias = sb("nbias", [128, NB * 8], F32)
        c_identb = sb("c_identb", [128, 128], BF16)
        c_identf = sb("c_identf", [128, 128], F32)
        c_maskf = sb("c_maskf", [128, 64], BF16)
        c_maskm = sb("c_maskm", [128, 64], BF16)
        c_umat = sb("c_umat", [128, 128], F32)
        c_col = sb("c_col", [128, 4], F32)
        c_gq = sb("c_gq", [128, 2], F32)
        c_gkv = sb("c_gkv", [128, 1], F32)
        c_np = sb("c_np", [128, 8], F32)
        c_onesb = sb("c_onesb", [128, 128], BF16)
        c_onesf = sb("c_onesf", [128, 128], F32)
        c_nhalf = sb("c_nhalf", [128, 512], F32)
        c_bfb = sb("c_bfb", [128, NB * 8], F32)
        wst = [sb("wst0", [128, 8 * 128], F32), sb("wst1", [128, 8 * 128], F32)]
        wb = [sb("wb%d" % i, [128, 8, 128], BF16) for i in range(4)]
        wuqn = sb("wuqn", [128, 2, 128], BF16)
        wuqr = sb("wuqr", [128, 2, 128], BF16)
        wukk = sb("wukk", [128, 512], BF16)
        wukv = sb("wukv", [128, 512], BF16)
        wflb = sb("wflb", [128, 8, 8], BF16)
        f32t = [sb("f32t0", [128, 1024], F32), sb("f32t1", [128, 1024], F32),
                sb("f32t2", [128, 512], F32), sb("f32t3", [128, 512], F32)]
        bft = [sb("bft%d" % i, [128, 1024], BF16) for i in range(3)]
        Pt = [sb("Pt%d" % i, [128, 512], BF16) for i in range(3)]
        rdb = [sb("rdb0", [128, 512], BF16), sb("rdb1", [128, 512], BF16)]
        small = sb("small", [128, 64], F32)
        ssq_all = sb("ssq_all", [128, 40], F32)
        lsp = sb("lsp", [128, NB * 8], F32)
        offs = sb("offs", [128, NB * 8], F32)

        ps = [psum("ps%d" % i, [128, 512], F32) for i in range(6)]
        es_A = ExitStack()
        pt = [es_A.enter_context(nc.psum_tensor("pt%d" % i, [128, 1024], BF16)) for i in range(2)]

        R = {}

        def res(name):
            if name not in R:
                R[name] = Res(name)
            return R[name]

        r_ps = [res("ps%d" % i) for i in range(6)]
        r_pt = [res("pt%d" % i) for i in range(2)]
        r_f32t = [res("f32t%d" % i) for i in range(4)]
        r_bft = [res("bft%d" % i) for i in range(3)]
        r_P = [res("P%d" % i) for i in range(3)]
        r_wst = [res("wst0"), res("wst1")]
        r_wb = [res("wb%d" % i) for i in range(4)]
        r_KB = [res("KB0"), res("KB1")]
        r_QB = [res("QB0"), res("QB1")]
        r_GB = [res("GB0"), res("GB1")]
        r_const = res("const")
        r_rdb = [res("rdb0"), res("rdb1")]

        B.op("pool", lambda: nc.gpsimd.memset(small[:], 0.0), writes=[r_const, res("small_init")])
        B.op("pool", lambda: nc.gpsimd.memset(c_nhalf[:], -0.5), writes=[r_const, res("nhalf")])
        B.op("pool", lambda: nc.gpsimd.memset(c_onesb[:], 1.0), writes=[r_const])
        B.op("pool", lambda: nc.gpsimd.memset(c_onesf[:], 1.0), writes=[r_const])
        xbufs = [(f32t[0][:], r_f32t[0]), (f32t[1][:], r_f32t[1]), (wst[0][:], r_wst[0]), (wst[1][:], r_wst[1])]
        for c in range(8):
            xbufs.append((attnT[:, c, :].bitcast(F32), res("xstage%d" % c)))
        NPRE = len(xbufs)
        r_identb = res("identb")
        B.dma("sp", "c", c_identb[:], identb[:, :], writes=[r_identb])
        r_np = res("np")
        B.dma("sp", "c", c_np[:], npre[:, :], writes=[r_np])
        r_wfl = res("wfl")
        B.dma("sp", "w", lsp[:, 0:64].rearrange("p (k c) -> p k c", k=8), w_fl[:, :, :], writes=[res("lsp")])
        B.op("dve", lambda: nc.vector.tensor_tensor(wflb[:], lsp[:, 0:64].rearrange("p (k c) -> p k c", k=8),
                                                    c_np[:, :].unsqueeze(2).broadcast_to([128, 8, 8]), ALU.mult),
             reads=[res("lsp"), r_np], writes=[r_wfl])
        for blk in range(NPRE):
            B.dma("sp", "x", xbufs[blk][0], xa[blk * 128:(blk + 1) * 128, :], writes=[xbufs[blk][1]])

        B.op("pool", lambda: nc.gpsimd.memset(VB[:], 1.0), writes=[res("VB"), res("VBo")])

        def rsqrt_pool(out_ap, in_ap, n, scr_ap, reads, writes, scr_res):
            B.op("pool", lambda: nc.gpsimd.tensor_scalar(scr_ap, in_ap, 1.0 / n, EPS, ALU.mult, ALU.add),
                 reads=reads, writes=[scr_res])
            B.op("pool", lambda: nc.gpsimd.tensor_tensor(out_ap, scr_ap, c_nhalf[:, 0:1], ALU.pow),
                 reads=[scr_res, res("nhalf")], writes=writes)

        r_hT = res("hT")
        def stage2(blk):
            ptb, r_ptb = pt[blk % 2], r_pt[blk % 2]
            dst = hT[:, :, blk * 128:(blk + 1) * 128]
            src = ptb[:].rearrange("p (c t) -> p c t", c=8)
            B.op("dve", lambda: nc.vector.tensor_copy(dst, src), reads=[r_ptb], writes=[r_hT])

        for blk in range(NB):
            xb_ = blk % 2
            xs, r_xs = xbufs[blk % len(xbufs)]
            xn, r_xn = bft[xb_], r_bft[xb_]
            junk, r_junk = bft[2], r_bft[2]
            if blk >= NPRE:
                B.dma("sp", "x", xs, xa[blk * 128:(blk + 1) * 128, :], writes=[r_xs])
            sscol = ssq_all[:, blk:blk + 1]
            r_ss = res("ss%d" % blk)
            B.op("act", lambda: nc.scalar.activation(out=junk[:], in_=xs, func=AF.Square, accum_out=sscol),
                 reads=[r_xs], writes=[r_junk, r_ss])
            rcol = small[:, blk:blk + 1]
            r_rc = res("rc%d" % blk)
            scr = small[:, 40 + (blk % 2):41 + (blk % 2)]
            rsqrt_pool(rcol, sscol, float(D), scr, [r_ss], [r_rc], res("scrA%d" % (blk % 2)))
            B.op("dve", lambda: nc.vector.tensor_scalar(xn[:], xs, rcol, None, ALU.mult),
                 reads=[r_xs, r_rc], writes=[r_xn])
            ptb, r_ptb = pt[xb_], r_pt[xb_]
            for c in range(8):
                B.op("pe", lambda c=c: nc.tensor.transpose(out=ptb[:, c * 128:(c + 1) * 128],
                                                           in_=xn[:, c * 128:(c + 1) * 128], identity=c_identb[:]),
                     reads=[r_xn, r_identb], writes=[r_ptb], signal=(c == 7))
            if blk >= 1:
                stage2(blk - 1)
        stage2(NB - 1)
        def cload(dst, src):
            B.dma("sp", "c", dst, src, writes=[r_const])

        cload(c_identf[:], identf[:, :])
        cload(c_maskf[:], maskf[:, :])
        cload(c_maskm[:], maskm[:, :])
        cload(c_umat[:], umat[:, :])
        cload(c_col[:], colc[:, :])
        cload(c_gq[:], gq[:, :])
        cload(c_gkv[:], gkv[:, :])
        cload(c_bfb[:], bfb[:, :])

        B.barrier()
        es_A.close()
        ps.append(psum("ps6", [128, 512], F32))
        ps.append(psum("ps7", [128, 512], F32))
        r_ps.append(res("ps6"))
        r_ps.append(res("ps7"))
        for hl in range(2):
            B.op("pool", lambda hl=hl: nc.gpsimd.memset(KB[hl][64:128, :], 0.0), writes=[r_KB[hl]])
            B.op("pool", lambda hl=hl: nc.gpsimd.memset(KB[hl][64:65, :], 1.0), writes=[r_KB[hl]])
            B.op("pool", lambda hl=hl: nc.gpsimd.memset(QB[hl][64:128, :], 0.0), writes=[r_QB[hl]])

        def own_cols(k, j):
            a = (8 * j + 1) * 128
            return hT[:, k, a:a + 1024].rearrange("p (n r) -> p n r", r=128)[:, :, 0:64]

        wl_state = {"i": 0}

        def load_w8(dst_bf, r_dst, src_ap):
            i = wl_state["i"] % 2
            wl_state["i"] += 1
            B.dma("sp", "w", wst[i][:].rearrange("p (k c) -> p k c", k=8), src_ap, writes=[r_wst[i]])
            B.op("dve", lambda: nc.vector.tensor_tensor(dst_bf[:], wst[i][:].rearrange("p (k c) -> p k c", k=8),
                                                        c_np[:, :].unsqueeze(2).broadcast_to([128, 8, 128]), ALU.mult),
                 reads=[r_wst[i], r_np], writes=[r_dst])

        pl = ps[0]
        for blk in range(NB):
            for k in range(8):
                B.op("pe", lambda blk=blk, k=k: nc.tensor.matmul(pl[:, blk * 8:(blk + 1) * 8],
                                                                 lhsT=hT[:, k, blk * 128:(blk + 1) * 128],
                                                                 rhs=wflb[:, k, :], start=(k == 0), stop=(k == 7)),
                     reads=[r_hT, r_wfl], writes=[r_ps[0]], signal=(k == 7 and blk == NB - 1))
        r_lsp = res("lsp")
        r_offs = res("offs")
        NL = NB * 8
        B.op("dve", lambda: nc.vector.tensor_tensor(lsp[:], pl[:, 0:NL], c_bfb[:], ALU.add),
             reads=[r_ps[0], r_const], writes=[r_lsp])
        B.op("act", lambda: nc.scalar.activation(out=offs[:], in_=lsp[:], func=AF.Exp, scale=-1.0),
             reads=[r_lsp], writes=[r_offs])
        B.op("act", lambda: nc.scalar.activation(out=lsp[:], in_=offs[:], func=AF.Ln, bias=1.0),
             reads=[r_offs], writes=[r_lsp])
        B.op("dve", lambda: nc.vector.tensor_scalar(lsp[:, 0:8], lsp[:, 0:8], c_col[:, 1:2], None, ALU.mult),
             reads=[r_lsp, r_const], writes=[r_lsp])
        B.op("pe", lambda: nc.tensor.matmul(ps[1][:, 0:NL], lhsT=c_umat[:], rhs=lsp[:], start=True, stop=True),
             reads=[r_lsp, r_const], writes=[r_ps[1]])
        B.op("pe", lambda: nc.tensor.matmul(ps[2][:, 0:NL], lhsT=c_onesf[:], rhs=lsp[:], start=True, stop=True),
             reads=[r_lsp, r_const], writes=[r_ps[2]])
        B.op("dve", lambda: nc.vector.memset(offs[:, 0:8], 0.0), reads=[r_offs], writes=[r_offs])
        B.op("dve", lambda: nc.vector.tensor_copy(offs[:, 8:NL], ps[2][:, 0:NL - 8]), reads=[r_ps[2]], writes=[r_offs])
        sc_src, r_sc_src, sc_dst, r_sc_dst = offs, r_offs, lsp, r_lsp
        for s_ in (1, 2, 4, 8, 16, 32):
            w_ = 8 * s_
            B.op("dve", lambda a=sc_src, d=sc_dst, w_=w_: nc.vector.tensor_tensor(d[:, w_:NL], a[:, w_:NL], a[:, 0:NL - w_], ALU.add),
                 reads=[r_sc_src], writes=[r_sc_dst])
            B.op("act", lambda a=sc_src, d=sc_dst, w_=w_: nc.scalar.copy(out=d[:, 0:w_], in_=a[:, 0:w_]),
                 reads=[r_sc_src], writes=[r_sc_dst])
            sc_src, r_sc_src, sc_dst, r_sc_dst = sc_dst, r_sc_dst, sc_src, r_sc_src
        assert sc_src is offs
        r_nb = res("nbias")
        B.op("dve", lambda: nc.vector.tensor_tensor(nbias[:], ps[1][:, 0:NL], offs[:], ALU.add),
             reads=[r_ps[1], r_offs], writes=[r_nb])
        cqr = ckvn
        r_cqr = res("cqr")
        for j in range(4):
            pj, r_pj = ps[3 + (j % 2)], r_ps[3 + (j % 2)]
            for d_ in range(8):
                blk = 8 * j + 1 + d_
                B.op("pe", lambda blk=blk, d_=d_, pj=pj: nc.tensor.transpose(out=pj[0:8, d_ * 64:(d_ + 1) * 64],
                                                                             in_=nbias[0:64, blk * 8:(blk + 1) * 8],
                                                                             identity=c_identf[0:64, 0:64]),
                     reads=[r_nb, r_const], writes=[r_pj], signal=(d_ == 7))
            B.op("dve", lambda j=j, pj=pj: nc.vector.tensor_scalar(cqr[96:104, j * 512:(j + 1) * 512], pj[0:8, :], -8.0, None, ALU.mult),
                 reads=[r_pj], writes=[r_cqr])
        B.op("dve", lambda: nc.vector.tensor_scalar(nbias[:, 0:8], nbias[:, 0:8], c_col[:, 0:1], None, ALU.add),
             reads=[r_nb, r_const], writes=[r_nb])

        def mm8(out_ap, r_out, wt, r_wt, rhs_fn):
            for k in range(8):
                B.op("pe", lambda k=k: nc.tensor.matmul(out_ap, lhsT=wt[:, k, :], rhs=rhs_fn(k), start=(k == 0), stop=(k == 7)),
                     reads=[r_hT, r_wt], writes=[r_out], signal=(k == 7))

        r_VB2, r_attn = [res("VB"), res("VBo")], res("attnT")
        S_BANKS = [0, 1, 6]
        O_BANKS = [2, 3]
        gcount = {"blk": 0, "tile": 0}
        pending_fin = []
        FIN_DEFER = 7

        def attention_pair(chunk, kind, wt_gate, r_wt_gate, hook=None):
            kdim = 128 if kind == "fox" else 96
            scale = FOX_SCALE if kind == "fox" else MLA_SCALE
            cmask = c_maskf if kind == "fox" else c_maskm
            seq = []
            for j in range(4):
                for hl in range(2):
                    blocks = [(blk, 0, 512, False) for blk in range(0, 8 * j + 1)]
                    blocks += [(8 * j + 1 + d_, 64 * d_, 512 - 64 * d_, True) for d_ in range(8)]
                    tno = gcount["tile"]
                    gcount["tile"] += 1
                    for bi, (blk, c0, N, diag) in enumerate(blocks):
                        g = gcount["blk"]
                        gcount["blk"] += 1
                        seq.append(dict(j=j, hl=hl, blk=blk, c0=c0, N=N, diag=diag, first=(bi == 0), last=(bi == len(blocks) - 1),
                                        sb=S_BANKS[g % 3], pi=g % 3, ob=O_BANKS[tno % 2], t=tno))
            nseq = len(seq)

            def qk(n):
                d = seq[n]
                sp_, r_sp = ps[d["sb"]], r_ps[d["sb"]]
                Kt, r_K = KB[d["hl"]], r_KB[d["hl"]]
                Qt, r_Q = QB[d["hl"]], r_QB[d["hl"]]
                blk, c0, N, diag = d["blk"], d["c0"], d["N"], d["diag"]
                q0 = d["j"] * 512
                B.op("pe", lambda: nc.tensor.matmul(sp_[:, 0:N], lhsT=Kt[0:kdim, blk * 128:(blk + 1) * 128],
                                                    rhs=Qt[0:kdim, q0 + c0:q0 + 512], start=True, stop=(not diag)),
                     reads=[r_K], late=[r_Q], late_w=[r_sp], signal=(not diag))
                if diag:
                    B.op("pe", lambda: nc.tensor.matmul(sp_[:, 0:64], lhsT=c_identb[:], rhs=cmask[:], start=False, stop=True),
                         reads=[r_const, r_identb], writes=[r_sp], signal=True)

            def ex(n):
                d = seq[n]
                sp_, r_sp = ps[d["sb"]], r_ps[d["sb"]]
                P_, r_P_ = Pt[d["pi"]], r_P[d["pi"]]
                blk, N = d["blk"], d["N"]
                if kind == "fox":
                    col = blk * 8 + chunk * 2 + d["hl"]
                    bias = nbias[:, col:col + 1]
                    rd = [r_sp, r_nb]
                elif blk == 0:
                    bias = c_col[:, 0:1]
                    rd = [r_sp, r_const]
                else:
                    bias = 0.0
                    rd = [r_sp]
                B.op("act", lambda: nc.scalar.activation(out=P_[:, 0:N], in_=sp_[:, 0:N], func=AF.Exp, bias=bias, scale=scale),
                     reads=rd, writes=[r_P_])

            def pv(n):
                d = seq[n]
                P_, r_P_ = Pt[d["pi"]], r_P[d["pi"]]
                po, r_po = ps[d["ob"]], r_ps[d["ob"]]
                blk, c0, N, hl = d["blk"], d["c0"], d["N"], d["hl"]
                B.op("pe", lambda: nc.tensor.matmul(po[:, c0:512], lhsT=VB[:, blk, hl * 65:hl * 65 + 128], rhs=P_[:, 0:N],
                                                    start=d["first"], stop=d["last"]),
                     reads=r_VB2, late=[r_P_], late_w=[r_po], signal=d["last"])

            def fin_a(d):
                po, r_po = ps[d["ob"]], r_ps[d["ob"]]
                rden, r_rden = rdb[d["t"] % 2], r_rdb[d["t"] % 2]

                def _recip():
                    with nc.allow_low_precision(reason="softmax normaliser broadcast operand in bf16 (fp32 reciprocal, rounded once)"):
                        return nc.vector.reciprocal(rden[64:65, 0:512], po[64:65, :])
                B.op("dve", _recip, reads=[r_po], writes=[r_rden])

            def fin_b(d):
                po, r_po = ps[d["ob"]], r_ps[d["ob"]]
                pbc, r_pbc = ps[4], r_ps[4]
                hl, q0 = d["hl"], d["j"] * 512
                GBj, r_GBj = GBt[d["j"] % 2], r_GB[d["j"] % 2]
                rden, r_rden = rdb[d["t"] % 2], r_rdb[d["t"] % 2]
                tg, r_tg = f32t[3], r_f32t[3]
                B.op("pe", lambda: nc.tensor.matmul(pbc[0:64, :], lhsT=c_onesb[64:65, 0:64], rhs=rden[64:65, 0:512], start=True, stop=True),
                     reads=[r_const], late=[r_rden], late_w=[r_pbc])
                B.op("dve", lambda: nc.vector.tensor_tensor(tg[0:64, 0:512], pbc[0:64, :], GBj[:, hl, :], ALU.mult),
                     reads=[r_pbc, r_GBj], writes=[r_tg])
                B.op("dve", lambda: nc.vector.tensor_tensor(attnT[hl * 64:(hl + 1) * 64, chunk, q0:q0 + 512], po[0:64, :], tg[0:64, 0:512], ALU.mult),
                     reads=[r_po, r_tg], writes=[r_attn])

            while pending_fin:
                pending_fin.pop(0)[1]()
            gate_due = None
            gate_mm(wt_gate, r_wt_gate, 0)
            gate_ep(0, GBt[0], r_GB[0])
            qk(0)
            if nseq > 1:
                qk(1)
            for n in range(nseq):
                d = seq[n]
                if d["first"] and d["hl"] == 0:
                    if d["j"] > 0:
                        gate_mm(wt_gate, r_wt_gate, d["j"])
                        gate_due = (n + 4, d["j"])
                    if d["j"] == 3 and hook is not None:
                        hook()
                if gate_due is not None and n >= gate_due[0]:
                    jj = gate_due[1]
                    gate_ep(jj, GBt[jj % 2], r_GB[jj % 2])
                    gate_due = None
                if n + 2 < nseq:
                    qk(n + 2)
                ex(n)
                pv(n)
                if pending_fin and n >= pending_fin[0][0] + FIN_DEFER:
                    pending_fin.pop(0)[1]()
                if d["last"]:
                    fin_a(d)
                    pending_fin.append((n, lambda d=d: fin_b(d)))
            for i in range(len(pending_fin)):
                pending_fin[i] = (-10**9, pending_fin[i][1])

        def gate_mm(wt, r_wt, j):
            pg, r_pg = ps[5], r_ps[5]
            mm8(pg[:], r_pg, wt, r_wt, lambda k: own_cols(k, j))

        def gate_ep(j, GBj, r_GBj):
            pg, r_pg = ps[5], r_ps[5]
            e_, r_e = f32t[j % 2], r_f32t[j % 2]
            B.op("act", lambda: nc.scalar.activation(out=e_[:, 0:512], in_=pg[:], func=AF.Exp, scale=-1.0), reads=[r_pg], writes=[r_e])
            B.op("dve", lambda: nc.vector.tensor_scalar(e_[:, 0:512], e_[:, 0:512], 1.0, None, ALU.add), reads=[r_e], writes=[r_e])
            B.op("dve", lambda: nc.vector.reciprocal(e_[:, 512:1024], e_[:, 0:512]), reads=[r_e], writes=[r_e])
            for hl in range(2):
                B.op("dve", lambda hl=hl: nc.vector.tensor_tensor(GBj[:, hl, :], pg[hl * 64:(hl + 1) * 64, :],
                                                                  e_[hl * 64:(hl + 1) * 64, 512:1024], ALU.mult),
                     reads=[r_pg, r_e], writes=[r_GBj])

        def evac_split(pa, r_pa, W, dst_fn, r_dsts, i):
            for hl in range(2):
                if hl == 0:
                    B.op("act", lambda hl=hl: nc.scalar.copy(out=dst_fn(hl), in_=pa[hl * 64:(hl + 1) * 64, 0:W]), reads=[r_pa], writes=[r_dsts[hl]])
                else:
                    B.op("dve", lambda hl=hl: nc.vector.tensor_copy(dst_fn(hl), pa[hl * 64:(hl + 1) * 64, 0:W]), reads=[r_pa], writes=[r_dsts[hl]])

        PB = [5, 7, 0, 1, 6]
        pb_state = {"i": 0}

        def next_pb():
            i = PB[pb_state["i"] % len(PB)]
            pb_state["i"] += 1
            return ps[i], r_ps[i]

        def v_proj(mm_fn):
            vi = 0
            for b0 in range(0, NB, 4):
                nb_ = min(4, NB - b0)
                pv_, r_pv = next_pb()
                for q in range(nb_):
                    mm_fn(pv_[:, q * 128:(q + 1) * 128], b0 + q, r_pv, q == nb_ - 1)
                dstv = VB[:, b0:b0 + nb_, 0:130].rearrange("p b (h c) -> p b h c", c=65)[:, :, :, 0:64]
                srcv = pv_[:, 0:nb_ * 128].rearrange("p (b h d) -> p b h d", b=nb_, h=2)
                if vi % 2 == 0:
                    B.op("act", lambda: nc.scalar.copy(out=dstv, in_=srcv), reads=[r_pv], writes=[r_VB2[0]])
                else:
                    B.op("dve", lambda: nc.vector.tensor_copy(dstv, srcv), reads=[r_pv], writes=[r_VB2[1]])
                vi += 1

        gstate = {"i": 0}

        def fox_weights(hp):
            for i in range(4):
                load_w8(wb[i], r_wb[i], w_fox[hp, :, :, i * 128:(i + 1) * 128])

        def phase_c_weights():
            for i in range(3):
                load_w8(wb[i], r_wb[i], w_c[:, :, (2 + i) * 128:(3 + i) * 128])

        fox_weights(0)
        for hp in range(4):
            for ti, (t0, W) in enumerate(TILES):
                pk, r_pk = next_pb()
                mm8(pk[:, 0:W], r_pk, wb[0], r_wb[0], lambda k: hT[:, k, t0:t0 + W])
                evac_split(pk, r_pk, W, lambda hl: KB[hl][0:64, t0:t0 + W], r_KB, ti)
            def fox_v_mm(out_ap, blk, r_bank, last):
                for k in range(8):
                    B.op("pe", lambda k=k: nc.tensor.matmul(out_ap, lhsT=hT[:, k, blk * 128:(blk + 1) * 128], rhs=wb[3][:, k, :],
                                                            start=(k == 0), stop=(k == 7)),
                         reads=[r_hT, r_wb[3]], writes=[r_bank], signal=(k == 7 and last))
            v_proj(fox_v_mm)
            for j in range(4):
                pq_, r_pq = next_pb()
                mm8(pq_[:], r_pq, wb[1], r_wb[1], lambda k: own_cols(k, j))
                evac_split(pq_, r_pq, 512, lambda hl: QB[hl][0:64, j * 512:(j + 1) * 512], r_QB, j)
            for hl in range(2):
                h = 2 * hp + hl
                B.dma("sp", "mv", QB[hl][64:65, :], cqr[96 + h:97 + h, 0:NOWN], reads=[r_cqr], writes=[r_QB[hl]])
            attention_pair(hp, "fox", wb[2], r_wb[2], hook=((lambda hp=hp: fox_weights(hp + 1)) if hp < 3 else phase_c_weights))

        r_ckvn, r_cqn = res("ckvn"), res("cqn")

        def norm_p1(psrc_list, W, par=0):
            single = (len(psrc_list) == 1)
            tmps = []
            for c, (pa, r_pa) in enumerate(psrc_list):
                if single:
                    tmp_ap, r_tmp = f32t[par][:, 0:W], r_f32t[par]
                    sq_ap, r_sq = bft[par][:, 0:W], r_bft[par]
                else:
                    tmp_ap, r_tmp = f32t[c][:, par * 512:par * 512 + W], r_f32t[c]
                    sq_ap, r_sq = bft[c][:, par * 512:par * 512 + W], r_bft[c]
                B.op("act", lambda pa=pa, tmp_ap=tmp_ap: nc.scalar.copy(out=tmp_ap, in_=pa), reads=[r_pa], writes=[r_tmp])
                B.op("dve", lambda pa=pa, tmp_ap=tmp_ap, sq_ap=sq_ap: nc.vector.tensor_tensor(sq_ap, pa, tmp_ap, ALU.mult),
                     reads=[r_pa, r_tmp], writes=[r_sq])
                tmps.append((tmp_ap, r_tmp, sq_ap, r_sq))
            return tmps

        def norm_p2(tmps, n_feat, gcols, dst_fn, r_dst, W, par=0):
            pq, r_pq = ps[4], r_ps[4]
            for c, (tmp_ap, r_tmp, sq_ap, r_sq) in enumerate(tmps):
                B.op("pe", lambda c=c, sq_ap=sq_ap: nc.tensor.matmul(pq[:, 0:W], lhsT=c_onesb[:], rhs=sq_ap,
                                                                     start=(c == 0), stop=(c == len(tmps) - 1)),
                     reads=[r_sq, r_const], writes=[r_pq], signal=(c == len(tmps) - 1))
            rr, r_rr = f32t[2 + par], r_f32t[2 + par]
            B.op("act", lambda: nc.scalar.activation(out=rr[:, 0:W], in_=pq[:, 0:W], func=AF.Ln, scale=1.0 / n_feat, bias=EPS),
                 reads=[r_pq], writes=[r_rr])
            B.op("act", lambda: nc.scalar.activation(out=rr[:, 0:W], in_=rr[:, 0:W], func=AF.Exp, scale=-0.5),
                 reads=[r_rr], writes=[r_rr])
            for c, (tmp_ap, r_tmp, sq_ap, r_sq) in enumerate(tmps):
                B.op("dve", lambda c=c, tmp_ap=tmp_ap: nc.vector.scalar_tensor_tensor(dst_fn(c), tmp_ap, gcols[:, c:c + 1], rr[:, 0:W],
                                                                                      ALU.mult, ALU.mult),
                     reads=[r_tmp, r_rr, r_const], writes=[r_dst])

        while pending_fin:
            pending_fin.pop(0)[1]()
        B.barrier()
        r_half = [[res("wst%d_h%d" % (i, h)) for h in range(2)] for i in range(2)]
        for ti, (t0, W) in enumerate(TILES):
            par = ti % 2
            b0, b1, b2 = (0, 1, 2) if par == 0 else (5, 6, 7)
            mm8(ps[b0][:, 0:W], r_ps[b0], wb[0], r_wb[0], lambda k: hT[:, k, t0:t0 + W])
            st_kv = norm_p1([(ps[b0][:, 0:W], r_ps[b0])], W, par)
            mm8(ps[b1][:, 0:W], r_ps[b1], wb[1], r_wb[1], lambda k: hT[:, k, t0:t0 + W])
            mm8(ps[b2][:, 0:W], r_ps[b2], wb[2], r_wb[2], lambda k: hT[:, k, t0:t0 + W])
            norm_p2(st_kv, 128.0, c_gkv, lambda c: ckvn[:, t0:t0 + W], r_ckvn, W, par)
            ct, r_ct = wst[0][:, par * 512:(par + 1) * 512], r_half[0][par]
            stt, r_stt = wst[1][:, par * 512:(par + 1) * 512], r_half[1][par]
            B.dma("sp", "tab", ct[64:96, 0:W], cck[:, t0:t0 + W], writes=[r_ct])
            B.dma("sp", "tab", stt[64:96, 0:W], ssk[:, t0:t0 + W], writes=[r_stt])
            B.op("dve", lambda: nc.vector.tensor_tensor(ct[64:96, 0:W], ps[b1][64:96, 0:W], ct[64:96, 0:W], ALU.mult),
                 reads=[r_ps[b1], r_ct], writes=[r_ct])
            B.op("dve", lambda: nc.vector.tensor_tensor(stt[64:96, 0:W], ps[b2][64:96, 0:W], stt[64:96, 0:W], ALU.mult),
                 reads=[r_ps[b2], r_stt], writes=[r_stt])
            for hl in range(2):
                B.op("dve", lambda hl=hl: nc.vector.tensor_tensor(KB[hl][64:96, t0:t0 + W], ct[64:96, 0:W], stt[64:96, 0:W], ALU.add),
                     reads=[r_ct, r_stt], writes=[r_KB[hl]])
        B.barrier()
        for i in range(2):
            load_w8(wb[i], r_wb[i], w_c[:, :, i * 128:(i + 1) * 128])
        r_wu = res("wu")
        r_wuq = res("wuq")
        for (dstw, srcw) in ((wukk, w_ukk), (wukv, w_ukv)):
            st, r_st = f32t[0], r_f32t[0]
            B.dma("sp", "w", st[:, 0:512], srcw[:, :], writes=[r_st])
            B.op("dve", lambda dstw=dstw, st=st: nc.vector.tensor_copy(dstw[:], st[:, 0:512]), reads=[r_st], writes=[r_wu])

        def mla_weights(hp):
            load_w8(wb[2], r_wb[2], w_mg[hp, :, :, :])
            for (dstw, srcw) in ((wuqn, w_uqn), (wuqr, w_uqr)):
                st, r_st = wst[1], r_wst[1]
                B.dma("sp", "w", st[:, 0:256].rearrange("p (c n) -> p c n", c=2), srcw[:, :, hp * 128:(hp + 1) * 128], writes=[r_st])
                B.op("dve", lambda dstw=dstw, st=st: nc.vector.tensor_copy(dstw[:].rearrange("p c n -> p (c n)"), st[:, 0:256]),
                     reads=[r_st], writes=[r_wuq])

        prev_q = None
        for j in range(4):
            par = j % 2
            b0, b1 = (0, 1) if par == 0 else (5, 6)
            mm8(ps[b0][:], r_ps[b0], wb[0], r_wb[0], lambda k: own_cols(k, j))
            mm8(ps[b1][:], r_ps[b1], wb[1], r_wb[1], lambda k: own_cols(k, j))
            st_q = norm_p1([(ps[b0][:], r_ps[b0]), (ps[b1][:], r_ps[b1])], 512, par)
            if prev_q is not None:
                norm_p2(prev_q[0], 256.0, c_gq, lambda c, jj=prev_q[1]: cqn[:, c, jj * 512:(jj + 1) * 512], r_cqn, 512, prev_q[2])
            prev_q = (st_q, j, par)
        norm_p2(prev_q[0], 256.0, c_gq, lambda c, jj=prev_q[1]: cqn[:, c, jj * 512:(jj + 1) * 512], r_cqn, 512, prev_q[2])

        r_wo = res("wo")

        def wo_view(k):
            if k < 4:
                return ckvn[:, k * 1024:(k + 1) * 1024]
            return cqn[:, (k - 4) // 2, ((k - 4) % 2) * 1024:((k - 4) % 2 + 1) * 1024]

        NXO = 8
        xo_slots = [(hT[:, k, 0:2048].bitcast(F32), res("xo%d" % k)) for k in range(NXO)]
        yo_slots = [(hT[:, k, 2048:4096].bitcast(F32), res("yo%d" % k)) for k in range(NXO)]
        o_state = {"hT_fenced": False}

        def load_xo(ob):
            xo, r_xo = xo_slots[ob % NXO]
            n0 = 1 + 2 * ob
            wr = [r_xo] + ([r_hT] if ob < NXO else [])
            B.dma("sp", "x", xo[0:64, :], xa[n0 * 128:n0 * 128 + 64, :], writes=wr)
            B.dma("sp", "x", xo[64:128, :], xa[(n0 + 1) * 128:(n0 + 1) * 128 + 64, :], writes=[r_xo])

        def phase_o_prefetch():
            for ob in range(NXO):
                load_xo(ob)

        mla_weights(0)
        for hp in range(4):
            for ti, (t0, W) in enumerate(TILES):
                pk, r_pk = next_pb()
                B.op("pe", lambda: nc.tensor.matmul(pk[:, 0:W], lhsT=wukk[:, hp * 128:(hp + 1) * 128], rhs=ckvn[:, t0:t0 + W], start=True, stop=True),
                     reads=[r_wu, r_ckvn], writes=[r_pk])
                evac_split(pk, r_pk, W, lambda hl: KB[hl][0:64, t0:t0 + W], r_KB, ti)
            def mla_v_mm(out_ap, blk, r_bank, last):
                B.op("pe", lambda: nc.tensor.matmul(out_ap, lhsT=ckvn[:, blk * 128:(blk + 1) * 128], rhs=wukv[:, hp * 128:(hp + 1) * 128],
                                                    start=True, stop=True),
                     reads=[r_wu, r_ckvn], writes=[r_bank], signal=last)
            v_proj(mla_v_mm)
            qsubs = [res("qrope_%s_%d" % (nm, p_)) for nm in ("t1", "t2", "t2b") for p_ in range(2)]
            B.op("dve", lambda: nc.vector.memset(small[:, 60:61], 0.0), writes=[r_f32t[0], r_f32t[1], res("qfence")] + qsubs)
            for j in range(4):
                qs = slice(j * 512, (j + 1) * 512)
                pn, r_pn = next_pb()
                pr, r_pr = next_pb()
                par = j % 2
                c0, c1 = par * 512, (1 - par) * 512
                r_t1, r_t2, r_t2b = res("qrope_t1_%d" % par), res("qrope_t2_%d" % par), res("qrope_t2b_%d" % par)
                t1 = f32t[0][0:64, c0:c0 + 512]
                t2 = f32t[1][64:128, c0:c0 + 512]
                t2b = f32t[1][0:64, c1:c1 + 512]
                B.dma("sp", "tab", t1, ccq[:, qs], writes=[r_t1])
                B.dma("sp", "tab", t2, ssq[:, qs], writes=[r_t2])
                for (pp, r_pp, wq) in ((pn, r_pn, wuqn), (pr, r_pr, wuqr)):
                    for c in range(2):
                        B.op("pe", lambda c=c, pp=pp, wq=wq: nc.tensor.matmul(pp[:], lhsT=wq[:, c, :], rhs=cqn[:, c, qs],
                                                                              start=(c == 0), stop=(c == 1)),
                             reads=[r_wuq, r_cqn], writes=[r_pp], signal=(c == 1))
                for hl in range(2):
                    B.op("act", lambda hl=hl: nc.scalar.copy(out=QB[hl][0:64, qs], in_=pn[hl * 64:(hl + 1) * 64, :]), reads=[r_pn], writes=[r_QB[hl]])
                B.op("dve", lambda: nc.vector.tensor_tensor(t1, pr[0:64, :], t1, ALU.mult),
                     reads=[r_pr, r_t1], writes=[r_t1])
                B.op("dve", lambda: nc.vector.tensor_tensor(t2b, pr[64:128, :], t2, ALU.mult),
                     reads=[r_pr, r_t2], writes=[r_t2b])
                for hl in range(2):
                    B.op("dve", lambda hl=hl: nc.vector.tensor_tensor(QB[hl][64:96, qs], f32t[0][hl * 32:(hl + 1) * 32, c0:c0 + 512],
                                                                      f32t[1][hl * 32:(hl + 1) * 32, c1:c1 + 512], ALU.add),
                         reads=[r_t1, r_t2b], writes=[r_QB[hl]])
            B.op("dve", lambda: nc.vector.memset(small[:, 60:61], 0.0), writes=[r_f32t[0], r_f32t[1], res("qfence")] + qsubs)
            if hp == 3:
                for k in range(8):
                    i = k % 2
                    B.dma("sp", "w", wst[i][:], w_out[:, k, :], writes=[r_wst[i]])
                    B.op("dve", lambda k=k, i=i: nc.vector.tensor_copy(wo_view(k), wst[i][:]), reads=[r_wst[i]], writes=[r_wo, r_ckvn if k < 4 else r_cqn])
            attention_pair(4 + hp, "mla", wb[2], r_wb[2], hook=((lambda hp=hp: mla_weights(hp + 1)) if hp < 3 else phase_o_prefetch))

        while pending_fin:
            pending_fin.pop(0)[1]()
        gp, r_gp = wst[0], r_wst[0]
        B.dma("sp", "c", gp[:], gpost[:, :], writes=[r_gp])
        for ob in range(16):
            q4 = ob % 4
            pyA, r_pyA = ps[q4 * 2], r_ps[q4 * 2]
            pyB, r_pyB = ps[q4 * 2 + 1], r_ps[q4 * 2 + 1]
            for (py, r_py, c0) in ((pyA, r_pyA, 0), (pyB, r_pyB, 512)):
                for c in range(8):
                    B.op("pe", lambda c=c, py=py, c0=c0: nc.tensor.matmul(py[:], lhsT=attnT[:, c, ob * 128:(ob + 1) * 128], rhs=wo_view(c)[:, c0:c0 + 512],
                                                                          start=(c == 0), stop=(c == 7)),
                         reads=[r_attn, r_wo], writes=[r_py], signal=(c == 7))
            junk, r_junk = bft[ob % 2], r_bft[ob % 2]
            o4 = 4 * q4
            ssA, ssB, rO, scrO = (small[:, o4 + q:o4 + q + 1] for q in range(4))
            r_s = res("sO%d" % q4)
            B.op("act", lambda: nc.scalar.activation(out=junk[:, 0:512], in_=pyA[:], func=AF.Square, accum_out=ssA),
                 reads=[r_pyA], writes=[r_junk, r_s])
            B.op("act", lambda: nc.scalar.activation(out=junk[:, 512:1024], in_=pyB[:], func=AF.Square, accum_out=ssB),
                 reads=[r_pyB], writes=[r_junk, r_s])
            B.op("act", lambda: nc.scalar.copy(out=small[:, 62:63], in_=small[:, 63:64]), reads=[res("small_init")], writes=[r_s, res("fence")])
            B.op("pool", lambda: nc.gpsimd.tensor_tensor(ssA, ssA, ssB, ALU.add), reads=[r_s], writes=[r_s])
            rsqrt_pool(rO, ssA, float(D), scrO, [r_s], [r_s], r_s)
            xo, r_xo = xo_slots[ob % NXO]
            yo, r_yo = yo_slots[ob % NXO]
            wy = [r_yo] + ([r_hT] if ob < NXO else [])
            B.op("dve", lambda: nc.vector.scalar_tensor_tensor(yo[:, 0:512], pyA[:], rO, gp[:, 0:512], ALU.mult, ALU.mult),
                 reads=[r_pyA, r_s, r_gp], writes=wy)
            B.op("dve", lambda: nc.vector.scalar_tensor_tensor(yo[:, 512:1024], pyB[:], rO, gp[:, 512:1024], ALU.mult, ALU.mult),
                 reads=[r_pyB, r_s, r_gp], writes=[r_yo])
            B.op("dve", lambda: nc.vector.tensor_tensor(yo, yo, xo, ALU.add), reads=[r_xo, r_yo], writes=[r_yo])
            if ob + NXO < 16:
                load_xo(ob + NXO)
            B.dma("sp", "out", yout[ob * 128:(ob + 1) * 128, :], yo, reads=[r_yo])
        B._wait("sp", [(k, v) for k, v in B.cnt.items() if k.startswith("dma") and v > 0])
        print("build: counts", B.cnt, "waits", B.nwaits, "sbuf left", nc.sbuf_bytes_remaining)
    return nc


def _bf16(a):
    return np.ascontiguousarray(a.astype(ml_dtypes.bfloat16))


def _rope_tables(pos):
    inv = (10000.0 ** (-np.arange(0, 32, 2, dtype=np.float32) / np.float32(32))).astype(np.float32)
    ang = pos.astype(np.float32)[:, None] * inv[None, :]
    c, s = np.cos(ang).astype(np.float32), np.sin(ang).astype(np.float32)
    cc = np.concatenate([c, c], axis=1).T
    ss = np.concatenate([-s, s], axis=1).T
    return np.ascontiguousarray(cc), np.ascontiguousarray(ss)


def _k8(w):
    return np.ascontiguousarray(w.reshape(8, 128, -1).transpose(1, 0, 2))


def make_inputs(x, meta, norm_pre, norm_post, w_in, b_f, q_norm, w_uq, kv_norm, w_ukv, w_out):
    x = np.asarray(x, np.float32)
    w_in = np.asarray(w_in, np.float32)[0]
    w_uq = np.asarray(w_uq, np.float32)[0]
    w_ukv = np.asarray(w_ukv, np.float32)[0]
    w_out = np.asarray(w_out, np.float32)[0]
    common = {}
    common["w_fl"] = _k8(w_in[:, O_FL:O_FL + 8])
    common["bfb"] = np.ascontiguousarray(np.tile(np.asarray(b_f, np.float32)[0][None, :], (128, NB)))
    npre = np.asarray(norm_pre, np.float32)[0].reshape(8, 128).T
    common["npre"] = np.ascontiguousarray(npre)
    z64 = np.zeros((D, 64), np.float32)
    z32 = np.zeros((D, 32), np.float32)
    kr = w_in[:, O_KR:O_KR + 32]
    krs = np.concatenate([kr[:, 16:32], kr[:, 0:16]], axis=1)
    wc = np.concatenate([w_in[:, O_CQ:O_CQ + 256], w_in[:, O_CKV:O_CKV + 128], z64, kr, z32, z64, krs, z32], axis=1)
    common["w_c"] = _k8(wc)
    wf = []
    for hp in range(4):
        sl = slice(hp * 128, (hp + 1) * 128)
        wf.append(_k8(np.concatenate([w_in[:, O_FK:O_FK + 512][:, sl], w_in[:, O_FQ:O_FQ + 512][:, sl],
                                      w_in[:, O_FG:O_FG + 512][:, sl], w_in[:, O_FV:O_FV + 512][:, sl]], axis=1)))
    common["w_fox"] = np.ascontiguousarray(np.stack(wf))
    common["w_mg"] = np.ascontiguousarray(np.stack([_k8(w_in[:, O_MG + hp * 128:O_MG + (hp + 1) * 128]) for hp in range(4)]))
    wq = w_uq.reshape(256, 8, 96)

    def _k2(w):
        return np.ascontiguousarray(w.reshape(2, 128, -1).transpose(1, 0, 2))
    common["w_uqn"] = _k2(wq[:, :, 0:64].reshape(256, 512))
    ra = wq[:, :, 64:96].reshape(256, 4, 64)
    rb = np.concatenate([wq[:, :, 80:96], wq[:, :, 64:80]], axis=2).reshape(256, 4, 64)
    common["w_uqr"] = _k2(np.concatenate([ra, rb], axis=2).reshape(256, 512))
    wkv = w_ukv.reshape(128, 8, 128)
    common["w_ukk"] = np.ascontiguousarray(wkv[:, :, 0:64].reshape(128, 512))
    common["w_ukv"] = np.ascontiguousarray(wkv[:, :, 64:128].reshape(128, 512))
    common["w_out"] = _k8(w_out)
    common["gq"] = np.ascontiguousarray(np.asarray(q_norm, np.float32)[0].reshape(2, 128).T)
    common["gkv"] = np.ascontiguousarray(np.asarray(kv_norm, np.float32)[0].reshape(128, 1))
    common["gpost"] = np.ascontiguousarray(np.tile(np.asarray(norm_post, np.float32)[0][None, :], (128, 1)))
    colc = np.zeros((128, 4), np.float32)
    colc[16:, 0] = NEG
    colc[:16, 1] = 1.0
    common["colc"] = colc
    common["identb"] = _bf16(np.eye(128, dtype=np.float32))
    common["identf"] = np.eye(128, dtype=np.float32)
    meta = np.asarray(meta, np.float32)
    in_maps = []
    r = np.arange(128)
    for core in range(8):
        b, g = core // 2, core % 2
        m = dict(common)
        xa = np.zeros((NB, 128, D), np.float32)
        xa[0, :16] = meta
        xc = x[b].reshape(32, 2, 64, D)
        xa[1:, 0:64] = xc[:, g]
        xa[1:, 64:128] = xc[:, 1 - g]
        m["xa"] = xa.reshape(TA, D)
        tpos = (r + 64 * g) % 128 if g == 1 else r
        if g == 1:
            tpos = np.where(r < 64, r + 64, r - 64)
        m["umat"] = np.ascontiguousarray((tpos[:, None] <= tpos[None, :]).astype(np.float32))
        mf = np.full((128, 64), NEG, np.float32)
        mm_ = np.full((128, 64), NEG, np.float32)
        cq_ = np.arange(64)
        own_ok = (r[:64, None] <= cq_[None, :])
        mf[:64] = np.where(own_ok, 0.0, NEG)
        mm_[:64] = 0.0
        if g == 1:
            mf[64:] = 0.0
            mm_[64:] = 0.0
        m["maskf"] = _bf16(mf)
        m["maskm"] = _bf16(mm_)
        pos = np.zeros((NB, 128), np.float32)
        pos[0] = np.arange(128)
        nn = np.arange(32)[:, None]
        rr = np.arange(128)[None, :]
        chunk = np.where(rr < 64, 2 * nn + g, 2 * nn + 1 - g)
        pos[1:] = 16 + 64 * chunk + (rr % 64)
        cc, ss = _rope_tables(pos.reshape(-1))
        m["cck"], m["ssk"] = cc, ss
        posq = pos[1:, 0:64].reshape(-1)
        cc, ss = _rope_tables(posq)
        m["ccq"] = np.ascontiguousarray(np.concatenate([cc, cc], axis=0))
        m["ssq"] = np.ascontiguousarray(np.concatenate([ss, ss], axis=0))
        in_maps.append(m)
    return in_maps


_CACHE = {}


def kernel(x, meta, norm_pre, norm_post, w_in, b_f, q_norm, w_uq, kv_norm, w_ukv, w_out):
    if "nc" not in _CACHE:
        _CACHE["nc"] = build_program()
    nc = _CACHE["nc"]
    in_maps = make_inputs(x, meta, norm_pre, norm_post, w_in, b_f, q_norm, w_uq, kv_norm, w_ukv, w_out)
    res = run_bass_kernel_spmd(nc, in_maps, core_ids=list(range(8)))
    out = np.zeros((4, 32, 2, 64, D), np.float32)
    for core in range(8):
        b, g = core // 2, core % 2
        out[b, :, g] = np.asarray(res.results[core]["yout"], np.float32).reshape(32, 64, D)
    return out.reshape(4, S, D)
```

```python
import os
import numpy as np
import ml_dtypes
from contextlib import ExitStack
import concourse.bass as bass
import concourse.mybir as mybir
from concourse.bass_utils import run_bass_kernel_spmd

F32 = mybir.dt.float32
BF16 = mybir.dt.bfloat16
AF = mybir.ActivationFunctionType
ALU = mybir.AluOpType

D = 1024
S = 4096
NB = 33
TA = NB * 128
NOWN = 2048
EPS = 1e-6
NEG = -30000.0
FOX_SCALE = 64 ** -0.5
MLA_SCALE = 96 ** -0.5
TILES = [(t * 512, min(512, TA - t * 512)) for t in range((TA + 511) // 512)]

O_FQ, O_FK, O_FV, O_FL, O_FG, O_CQ, O_CKV, O_KR, O_MG = 0, 512, 1024, 1536, 1544, 2056, 2312, 2440, 2472


class Res:
    __slots__ = ("name", "lw", "rd")

    def __init__(self, name):
        self.name = name
        self.lw = None
        self.rd = {}


class Builder:
    def __init__(self, nc, es):
        self.nc = nc
        self.E = {"pe": nc.tensor, "act": nc.scalar, "dve": nc.vector, "pool": nc.gpsimd, "sp": nc.sync}
        self.sem = {}
        self.cnt = {}
        for e in ("pe", "act", "dve", "pool"):
            self.sem[e] = es.enter_context(nc.semaphore("s_" + e))
            self.cnt[e] = 0
        self.es = es
        self.known = {e: {} for e in self.E}
        self.snap = {}
        self.nwaits = 0

    NSLOT = 40

    def stream(self, name):
        pass

    def init_dma_slots(self):
        self.dma_i = 0
        for i in range(self.NSLOT):
            k = "dma%d" % i
            self.sem[k] = self.es.enter_context(self.nc.semaphore("d_%d" % i))
            self.cnt[k] = 0

    def _wait(self, eng, deps):
        kn = self.known[eng]
        best = {}
        for (e, v) in deps:
            if v > best.get(e, 0):
                best[e] = v
        for e, v in best.items():
            if kn.get(e, 0) >= v:
                continue
            assert v <= self.cnt[e], (eng, e, v, self.cnt[e])
            self.E[eng].wait_ge(self.sem[e], v)
            self.nwaits += 1
            kn[e] = v
            sn = self.snap.get((e, v))
            if sn:
                for k2, v2 in sn.items():
                    if kn.get(k2, 0) < v2:
                        kn[k2] = v2

    def _deps(self, eng, reads, writes):
        deps = []
        for r in reads:
            if r.lw is not None:
                if not (r.lw[0] == eng and eng == "pe"):
                    deps.append(r.lw)
        for w in writes:
            if w.lw is not None and not (w.lw[0] == eng and eng == "pe"):
                deps.append(w.lw)
            for e, v in w.rd.items():
                if not (e == eng and eng == "pe"):
                    deps.append((e, v))
        return deps

    def _commit(self, ev, reads, writes):
        for w in writes:
            w.lw = ev
            w.rd = {}
        for r in reads:
            if r.rd.get(ev[0], 0) < ev[1]:
                r.rd[ev[0]] = ev[1]

    def op(self, eng, fn, reads=(), writes=(), signal=True, late=(), late_w=()):
        self._wait(eng, self._deps(eng, reads, writes))
        ldeps = self._deps(eng, late, late_w)
        kn = self.known[eng]
        best = {}
        for (e, v) in ldeps:
            if v > best.get(e, 0) and kn.get(e, 0) < v:
                best[e] = v
        attach = None
        if len(best) == 1:
            attach = list(best.items())[0]
        elif len(best) > 1:
            self._wait(eng, ldeps)
        inst = fn()
        if attach is not None:
            e, v = attach
            assert v <= self.cnt[e], (eng, e, v, self.cnt[e])
            inst._wait_ge(self.sem[e], v)
            self.nwaits += 1
            kn[e] = v
            sn = self.snap.get((e, v))
            if sn:
                for k2, v2 in sn.items():
                    if kn.get(k2, 0) < v2:
                        kn[k2] = v2
        n = self.cnt[eng] + 1
        if signal:
            inst.then_inc(self.sem[eng], 1)
            self.cnt[eng] = n
            self.snap[(eng, n)] = dict(self.known[eng])
        self._commit((eng, n), list(reads) + list(late), list(writes) + list(late_w))
        return inst

    def dma(self, q, stream, out, in_, reads=(), writes=()):
        k = "dma%d" % (self.dma_i % self.NSLOT)
        self.dma_i += 1
        deps = self._deps(q, reads, writes)
        if self.cnt[k] > 0:
            deps.append((k, self.cnt[k]))
        self._wait(q, deps)
        inst = self.E[q].dma_start(out=out, in_=in_)
        inst.then_inc(self.sem[k], 16)
        n = self.cnt[k] + 16
        self.cnt[k] = n
        self.snap[(k, n)] = dict(self.known[q])
        self._commit((k, n), reads, writes)
        return inst

    def barrier(self):
        for eng in self.E:
            self._wait(eng, [(e, v) for e, v in self.cnt.items() if v > 0 and e != eng])


def build_program(debug=False):
    nc = bass.Bass("TRN2", target_bir_lowering=False)

    def din(name, shape, dt=F32):
        return nc.dram_tensor(name, list(shape), dt, kind="ExternalInput").ap()

    xa = din("xa", [TA, D])
    w_fl = din("w_fl", [128, 8, 8])
    bfb = din("bfb", [128, NB * 8])
    npre = din("npre", [128, 8])
    w_c = din("w_c", [128, 8, 640])
    w_fox = din("w_fox", [4, 128, 8, 512])
    w_mg = din("w_mg", [4, 128, 8, 128])
    w_uqn = din("w_uqn", [128, 2, 512])
    w_uqr = din("w_uqr", [128, 2, 512])
    w_ukk = din("w_ukk", [128, 512])
    w_ukv = din("w_ukv", [128, 512])
    w_out = din("w_out", [128, 8, 1024])
    gq = din("gq", [128, 2])
    gkv = din("gkv", [128, 1])
    gpost = din("gpost", [128, D])
    umat = din("umat", [128, 128])
    colc = din("colc", [128, 4])
    maskf = din("maskf", [128, 64], BF16)
    maskm = din("maskm", [128, 64], BF16)
    identb = din("identb", [128, 128], BF16)
    identf = din("identf", [128, 128])
    cck = din("cck", [32, TA])
    ssk = din("ssk", [32, TA])
    ccq = din("ccq", [64, NOWN])
    ssq = din("ssq", [64, NOWN])
    yout = nc.dram_tensor("yout", [NOWN, D], F32, kind="ExternalOutput").ap()
    dbg = {}

    es = ExitStack()
    with es:
        B = Builder(nc, es)
        B.init_dma_slots()

        def sb(name, shape, dt):
            return es.enter_context(nc.sbuf_tensor(name, list(shape), dt))

        def psum(name, shape, dt):
            return es.enter_context(nc.psum_tensor(name, list(shape), dt))

        hT = sb("hT", [128, 8, TA], BF16)
        attnT = sb("attnT", [128, 8, NOWN], BF16)
        KB = [sb("KB0", [128, TA], BF16), sb("KB1", [128, TA], BF16)]
        VB = sb("VB", [128, NB, 193], BF16)
        QB = [sb("QB0", [128, NOWN], BF16), sb("QB1", [128, NOWN], BF16)]
        GBt = [sb("GB0", [64, 2, 512], BF16), sb("GB1", [64, 2, 512], BF16)]
        ckvn = sb("ckvn", [128, TA], BF16)
        cqn = sb("cqn", [128, 2, NOWN], BF16)
        nbias = sb("nbias", [128, NB * 8], F32)
        c_identb = sb("c_identb", [128, 128], BF16)
        c_identf = sb("c_identf", [128, 128], F32)
        c_maskf = sb("c_maskf", [128, 64], BF16)
        c_maskm = sb("c_maskm", [128, 64], BF16)
        c_umat = sb("c_umat", [128, 128], F32)
        c_col = sb("c_col", [128, 4], F32)
        c_gq = sb("c_gq", [128, 2], F32)
        c_gkv = sb("c_gkv", [128, 1], F32)
        c_np = sb("c_np", [128, 8], F32)
        c_onesb = sb("c_onesb", [128, 128], BF16)
        c_onesf = sb("c_onesf", [128, 128], F32)
        c_nhalf = sb("c_nhalf", [128, 512], F32)
        c_bfb = sb("c_bfb", [128, NB * 8], F32)
        wst = [sb("wst0", [128, 8 * 128], F32), sb("wst1", [128, 8 * 128], F32)]
        wb = [sb("wb%d" % i, [128, 8, 128], BF16) for i in range(4)]
        wuqn = sb("wuqn", [128, 2, 128], BF16)
        wuqr = sb("wuqr", [128, 2, 128], BF16)
        wukk = sb("wukk", [128, 512], BF16)
        wukv = sb("wukv", [128, 512], BF16)
        wflb = sb("wflb", [128, 8, 8], BF16)
        f32t = [sb("f32t0", [128, 1024], F32), sb("f32t1", [128, 1024], F32),
                sb("f32t2", [128, 512], F32), sb("f32t3", [128, 512], F32)]
        bft = [sb("bft%d" % i, [128, 1024], BF16) for i in range(3)]
        Pt = [sb("Pt%d" % i, [128, 512], BF16) for i in range(4)]
        rdb = [sb("rdb0", [128, 512], BF16), sb("rdb1", [128, 512], BF16)]
        small = sb("small", [128, 64], F32)
        ssq_all = sb("ssq_all", [128, 40], F32)
        lsp = sb("lsp", [128, NB * 8], F32)
        offs = sb("offs", [128, NB * 8], F32)

        ps = [psum("ps%d" % i, [128, 512], F32) for i in range(6)]
        es_A = ExitStack()
        pt = [es_A.enter_context(nc.psum_tensor("pt%d" % i, [128, 1024], BF16)) for i in range(2)]

        R = {}

        def res(name):
            if name not in R:
                R[name] = Res(name)
            return R[name]

        r_ps = [res("ps%d" % i) for i in range(6)]
        r_pt = [res("pt%d" % i) for i in range(2)]
        r_f32t = [res("f32t%d" % i) for i in range(4)]
        r_bft = [res("bft%d" % i) for i in range(3)]
        r_P = [res("P%d" % i) for i in range(4)]
        r_wst = [res("wst0"), res("wst1")]
        r_wb = [res("wb%d" % i) for i in range(4)]
        r_KB = [res("KB0"), res("KB1")]
        r_QB = [res("QB0"), res("QB1")]
        r_GB = [res("GB0"), res("GB1")]
        r_const = res("const")
        r_rdb = [res("rdb0"), res("rdb1")]

        B.op("pool", lambda: nc.gpsimd.memset(small[:], 0.0), writes=[r_const, res("small_init")])
        B.op("pool", lambda: nc.gpsimd.memset(c_nhalf[:], -0.5), writes=[r_const, res("nhalf")])
        B.op("pool", lambda: nc.gpsimd.memset(c_onesb[:], 1.0), writes=[r_const])
        B.op("pool", lambda: nc.gpsimd.memset(c_onesf[:], 1.0), writes=[r_const])
        xbufs = [(f32t[0][:], r_f32t[0]), (f32t[1][:], r_f32t[1]), (wst[0][:], r_wst[0]), (wst[1][:], r_wst[1])]
        for c in range(8):
            xbufs.append((attnT[:, c, :].bitcast(F32), res("xstage%d" % c)))
        NPRE = len(xbufs)
        r_identb = res("identb")
        B.dma("sp", "c", c_identb[:], identb[:, :], writes=[r_identb])
        r_np = res("np")
        B.dma("sp", "c", c_np[:], npre[:, :], writes=[r_np])
        r_wfl = res("wfl")
        B.dma("sp", "w", lsp[:, 0:64].rearrange("p (k c) -> p k c", k=8), w_fl[:, :, :], writes=[res("lsp")])
        B.op("dve", lambda: nc.vector.tensor_tensor(wflb[:], lsp[:, 0:64].rearrange("p (k c) -> p k c", k=8),
                                                    c_np[:, :].unsqueeze(2).broadcast_to([128, 8, 8]), ALU.mult),
             reads=[res("lsp"), r_np], writes=[r_wfl])
        for blk in range(NPRE):
            B.dma("sp", "x", xbufs[blk][0], xa[blk * 128:(blk + 1) * 128, :], writes=[xbufs[blk][1]])

        B.op("pool", lambda: nc.gpsimd.memset(VB[:], 1.0), writes=[res("VB"), res("VBo")])

        def rsqrt_pool(out_ap, in_ap, n, scr_ap, reads, writes, scr_res):
            B.op("pool", lambda: nc.gpsimd.tensor_scalar(scr_ap, in_ap, 1.0 / n, EPS, ALU.mult, ALU.add),
                 reads=reads, writes=[scr_res])
            B.op("pool", lambda: nc.gpsimd.tensor_tensor(out_ap, scr_ap, c_nhalf[:, 0:1], ALU.pow),
                 reads=[scr_res, res("nhalf")], writes=writes)

        r_hT = res("hT")
        def stage2(blk):
            ptb, r_ptb = pt[blk % 2], r_pt[blk % 2]
            dst = hT[:, :, blk * 128:(blk + 1) * 128]
            src = ptb[:].rearrange("p (c t) -> p c t", c=8)
            B.op("dve", lambda: nc.vector.tensor_copy(dst, src), reads=[r_ptb], writes=[r_hT])

        for blk in range(NB):
            xb_ = blk % 2
            xs, r_xs = xbufs[blk % len(xbufs)]
            xn, r_xn = bft[xb_], r_bft[xb_]
            junk, r_junk = bft[2], r_bft[2]
            if blk >= NPRE:
                B.dma("sp", "x", xs, xa[blk * 128:(blk + 1) * 128, :], writes=[r_xs])
            sscol = ssq_all[:, blk:blk + 1]
            r_ss = res("ss%d" % blk)
            B.op("act", lambda: nc.scalar.activation(out=junk[:], in_=xs, func=AF.Square, accum_out=sscol),
                 reads=[r_xs], writes=[r_junk, r_ss])
            B.op("act", lambda: nc.scalar.copy(out=small[:, 62:63], in_=small[:, 63:64]), reads=[res("small_init")], writes=[r_ss, res("fence")])
            rcol = small[:, blk:blk + 1]
            r_rc = res("rc%d" % blk)
            scr = small[:, 40 + (blk % 2):41 + (blk % 2)]
            rsqrt_pool(rcol, sscol, float(D), scr, [r_ss], [r_rc], res("scrA%d" % (blk % 2)))
            B.op("dve", lambda: nc.vector.tensor_scalar(xn[:], xs, rcol, None, ALU.mult),
                 reads=[r_xs, r_rc], writes=[r_xn])
            ptb, r_ptb = pt[xb_], r_pt[xb_]
            for c in range(8):
                B.op("pe", lambda c=c: nc.tensor.transpose(out=ptb[:, c * 128:(c + 1) * 128],
                                                           in_=xn[:, c * 128:(c + 1) * 128], identity=c_identb[:]),
                     reads=[r_xn, r_identb], writes=[r_ptb], signal=(c == 7))
            if blk >= 1:
                stage2(blk - 1)
        stage2(NB - 1)
        def cload(dst, src):
            B.dma("sp", "c", dst, src, writes=[r_const])

        cload(c_identf[:], identf[:, :])
        cload(c_maskf[:], maskf[:, :])
        cload(c_maskm[:], maskm[:, :])
        cload(c_umat[:], umat[:, :])
        cload(c_col[:], colc[:, :])
        cload(c_gq[:], gq[:, :])
        cload(c_gkv[:], gkv[:, :])
        cload(c_bfb[:], bfb[:, :])

        B.barrier()
        es_A.close()
        ps.append(psum("ps6", [128, 512], F32))
        ps.append(psum("ps7", [128, 512], F32))
        r_ps.append(res("ps6"))
        r_ps.append(res("ps7"))
        for hl in range(2):
            B.op("pool", lambda hl=hl: nc.gpsimd.memset(KB[hl][64:128, :], 0.0), writes=[r_KB[hl]])
            B.op("pool", lambda hl=hl: nc.gpsimd.memset(KB[hl][64:65, :], 1.0), writes=[r_KB[hl]])
            B.op("pool", lambda hl=hl: nc.gpsimd.memset(QB[hl][64:128, :], 0.0), writes=[r_QB[hl]])

        def own_cols(k, j):
            a = (8 * j + 1) * 128
            return hT[:, k, a:a + 1024].rearrange("p (n r) -> p n r", r=128)[:, :, 0:64]

        wl_state = {"i": 0}

        def load_w8(dst_bf, r_dst, src_ap):
            i = wl_state["i"] % 2
            wl_state["i"] += 1
            B.dma("sp", "w", wst[i][:].rearrange("p (k c) -> p k c", k=8), src_ap, writes=[r_wst[i]])
            B.op("dve", lambda: nc.vector.tensor_tensor(dst_bf[:], wst[i][:].rearrange("p (k c) -> p k c", k=8),
                                                        c_np[:, :].unsqueeze(2).broadcast_to([128, 8, 128]), ALU.mult),
                 reads=[r_wst[i], r_np], writes=[r_dst])

        pl = ps[0]
        for blk in range(NB):
            for k in range(8):
                B.op("pe", lambda blk=blk, k=k: nc.tensor.matmul(pl[:, blk * 8:(blk + 1) * 8],
                                                                 lhsT=hT[:, k, blk * 128:(blk + 1) * 128],
                                                                 rhs=wflb[:, k, :], start=(k == 0), stop=(k == 7)),
                     reads=[r_hT, r_wfl], writes=[r_ps[0]], signal=(k == 7 and blk == NB - 1))
        r_lsp = res("lsp")
        r_offs = res("offs")
        NL = NB * 8
        B.op("dve", lambda: nc.vector.tensor_tensor(lsp[:], pl[:, 0:NL], c_bfb[:], ALU.add),
             reads=[r_ps[0], r_const], writes=[r_lsp])
        B.op("act", lambda: nc.scalar.activation(out=offs[:], in_=lsp[:], func=AF.Exp, scale=-1.0),
             reads=[r_lsp], writes=[r_offs])
        B.op("act", lambda: nc.scalar.activation(out=lsp[:], in_=offs[:], func=AF.Ln, bias=1.0),
             reads=[r_offs], writes=[r_lsp])
        B.op("dve", lambda: nc.vector.tensor_scalar(lsp[:, 0:8], lsp[:, 0:8], c_col[:, 1:2], None, ALU.mult),
             reads=[r_lsp, r_const], writes=[r_lsp])
        B.op("pe", lambda: nc.tensor.matmul(ps[1][:, 0:NL], lhsT=c_umat[:], rhs=lsp[:], start=True, stop=True),
             reads=[r_lsp, r_const], writes=[r_ps[1]])
        B.op("pe", lambda: nc.tensor.matmul(ps[2][:, 0:NL], lhsT=c_onesf[:], rhs=lsp[:], start=True, stop=True),
             reads=[r_lsp, r_const], writes=[r_ps[2]])
        B.op("dve", lambda: nc.vector.memset(offs[:, 0:8], 0.0), reads=[r_offs], writes=[r_offs])
        B.op("dve", lambda: nc.vector.tensor_copy(offs[:, 8:NL], ps[2][:, 0:NL - 8]), reads=[r_ps[2]], writes=[r_offs])
        sc_src, r_sc_src, sc_dst, r_sc_dst = offs, r_offs, lsp, r_lsp
        for s_ in (1, 2, 4, 8, 16, 32):
            w_ = 8 * s_
            B.op("dve", lambda a=sc_src, d=sc_dst, w_=w_: nc.vector.tensor_tensor(d[:, w_:NL], a[:, w_:NL], a[:, 0:NL - w_], ALU.add),
                 reads=[r_sc_src], writes=[r_sc_dst])
            B.op("act", lambda a=sc_src, d=sc_dst, w_=w_: nc.scalar.copy(out=d[:, 0:w_], in_=a[:, 0:w_]),
                 reads=[r_sc_src], writes=[r_sc_dst])
            sc_src, r_sc_src, sc_dst, r_sc_dst = sc_dst, r_sc_dst, sc_src, r_sc_src
        assert sc_src is offs
        r_nb = res("nbias")
        B.op("dve", lambda: nc.vector.tensor_tensor(nbias[:], ps[1][:, 0:NL], offs[:], ALU.add),
             reads=[r_ps[1], r_offs], writes=[r_nb])
        cqr = ckvn
        r_cqr = res("cqr")
        for j in range(4):
            pj, r_pj = ps[3 + (j % 2)], r_ps[3 + (j % 2)]
            for d_ in range(8):
                blk = 8 * j + 1 + d_
                B.op("pe", lambda blk=blk, d_=d_, pj=pj: nc.tensor.transpose(out=pj[0:8, d_ * 64:(d_ + 1) * 64],
                                                                             in_=nbias[0:64, blk * 8:(blk + 1) * 8],
                                                                             identity=c_identf[0:64, 0:64]),
                     reads=[r_nb, r_const], writes=[r_pj], signal=(d_ == 7))
            B.op("dve", lambda j=j, pj=pj: nc.vector.tensor_scalar(cqr[96:104, j * 512:(j + 1) * 512], pj[0:8, :], -8.0, None, ALU.mult),
                 reads=[r_pj], writes=[r_cqr])
        B.op("dve", lambda: nc.vector.tensor_scalar(nbias[:, 0:8], nbias[:, 0:8], c_col[:, 0:1], None, ALU.add),
             reads=[r_nb, r_const], writes=[r_nb])

        def mm8(out_ap, r_out, wt, r_wt, rhs_fn):
            for k in range(8):
                B.op("pe", lambda k=k: nc.tensor.matmul(out_ap, lhsT=wt[:, k, :], rhs=rhs_fn(k), start=(k == 0), stop=(k == 7)),
                     reads=[r_hT, r_wt], writes=[r_out], signal=(k == 7))

        r_VB2, r_attn = [res("VB"), res("VBo")], res("attnT")
        S_BANKS = [0, 1, 6, 7]
        O_BANKS = [2, 3]
        gcount = {"blk": 0, "tile": 0}
        pending_fin = []
        FIN_DEFER = 7

        def attention_pair(chunk, kind, wt_gate, r_wt_gate, hook=None):
            kdim = 128 if kind == "fox" else 96
            scale = FOX_SCALE if kind == "fox" else MLA_SCALE
            cmask = c_maskf if kind == "fox" else c_maskm
            seq = []
            for j in range(4):
                for hl in range(2):
                    blocks = [(blk, 0, 512, False) for blk in range(0, 8 * j + 1)]
                    blocks += [(8 * j + 1 + d_, 64 * d_, 512 - 64 * d_, True) for d_ in range(8)]
                    tno = gcount["tile"]
                    gcount["tile"] += 1
                    for bi, (blk, c0, N, diag) in enumerate(blocks):
                        g = gcount["blk"]
                        gcount["blk"] += 1
                        seq.append(dict(j=j, hl=hl, blk=blk, c0=c0, N=N, diag=diag, first=(bi == 0), last=(bi == len(blocks) - 1),
                                        sb=S_BANKS[g % 4], pi=g % 4, ob=O_BANKS[tno % 2], t=tno))
            nseq = len(seq)

            def qk(n):
                d = seq[n]
                sp_, r_sp = ps[d["sb"]], r_ps[d["sb"]]
                Kt, r_K = KB[d["hl"]], r_KB[d["hl"]]
                Qt, r_Q = QB[d["hl"]], r_QB[d["hl"]]
                blk, c0, N, diag = d["blk"], d["c0"], d["N"], d["diag"]
                q0 = d["j"] * 512
                B.op("pe", lambda: nc.tensor.matmul(sp_[:, 0:N], lhsT=Kt[0:kdim, blk * 128:(blk + 1) * 128],
                                                    rhs=Qt[0:kdim, q0 + c0:q0 + 512], start=True, stop=(not diag)),
                     reads=[r_K], late=[r_Q], late_w=[r_sp], signal=(not diag))
                if diag:
                    B.op("pe", lambda: nc.tensor.matmul(sp_[:, 0:64], lhsT=c_identb[:], rhs=cmask[:], start=False, stop=True),
                         reads=[r_const, r_identb], writes=[r_sp], signal=True)

            def ex(n):
                d = seq[n]
                sp_, r_sp = ps[d["sb"]], r_ps[d["sb"]]
                P_, r_P_ = Pt[d["pi"]], r_P[d["pi"]]
                blk, N = d["blk"], d["N"]
                if kind == "fox":
                    col = blk * 8 + chunk * 2 + d["hl"]
                    bias = nbias[:, col:col + 1]
                    rd = [r_sp, r_nb]
                elif blk == 0:
                    bias = c_col[:, 0:1]
                    rd = [r_sp, r_const]
                else:
                    bias = 0.0
                    rd = [r_sp]
                B.op("act", lambda: nc.scalar.activation(out=P_[:, 0:N], in_=sp_[:, 0:N], func=AF.Exp, bias=bias, scale=scale),
                     reads=rd, writes=[r_P_])

            def pv(n):
                d = seq[n]
                P_, r_P_ = Pt[d["pi"]], r_P[d["pi"]]
                po, r_po = ps[d["ob"]], r_ps[d["ob"]]
                blk, c0, N, hl = d["blk"], d["c0"], d["N"], d["hl"]
                B.op("pe", lambda: nc.tensor.matmul(po[:, c0:512], lhsT=VB[:, blk, hl * 65:hl * 65 + 128], rhs=P_[:, 0:N],
                                                    start=d["first"], stop=d["last"]),
                     reads=r_VB2, late=[r_P_], late_w=[r_po], signal=d["last"])

            def fin_a(d):
                po, r_po = ps[d["ob"]], r_ps[d["ob"]]
                rden, r_rden = rdb[d["t"] % 2], r_rdb[d["t"] % 2]

                def _recip():
                    with nc.allow_low_precision(reason="softmax normaliser broadcast operand in bf16 (fp32 reciprocal, rounded once)"):
                        return nc.vector.reciprocal(rden[64:65, 0:512], po[64:65, :])
                B.op("dve", _recip, reads=[r_po], writes=[r_rden])

            def fin_b(d):
                po, r_po = ps[d["ob"]], r_ps[d["ob"]]
                pbc, r_pbc = ps[4], r_ps[4]
                hl, q0 = d["hl"], d["j"] * 512
                GBj, r_GBj = GBt[d["j"] % 2], r_GB[d["j"] % 2]
                rden, r_rden = rdb[d["t"] % 2], r_rdb[d["t"] % 2]
                tg, r_tg = f32t[3], r_f32t[3]
                B.op("pe", lambda: nc.tensor.matmul(pbc[0:64, :], lhsT=c_onesb[64:65, 0:64], rhs=rden[64:65, 0:512], start=True, stop=True),
                     reads=[r_const], late=[r_rden], late_w=[r_pbc])
                B.op("dve", lambda: nc.vector.tensor_tensor(tg[0:64, 0:512], pbc[0:64, :], GBj[:, hl, :], ALU.mult),
                     reads=[r_pbc, r_GBj], writes=[r_tg])
                B.op("dve", lambda: nc.vector.tensor_tensor(attnT[hl * 64:(hl + 1) * 64, chunk, q0:q0 + 512], po[0:64, :], tg[0:64, 0:512], ALU.mult),
                     reads=[r_po, r_tg], writes=[r_attn])

            while pending_fin:
                pending_fin.pop(0)[1]()
            gate_due = None
            gate_mm(wt_gate, r_wt_gate, 0)
            gate_ep(0, GBt[0], r_GB[0])
            qk(0)
            if nseq > 1:
                qk(1)
            if nseq > 2:
                qk(2)
            for n in range(nseq):
                d = seq[n]
                if d["first"] and d["hl"] == 0:
                    if d["j"] > 0:
                        gate_mm(wt_gate, r_wt_gate, d["j"])
                        gate_due = (n + 4, d["j"])
                    if d["j"] == 3 and hook is not None:
                        hook()
                if gate_due is not None and n >= gate_due[0]:
                    jj = gate_due[1]
                    gate_ep(jj, GBt[jj % 2], r_GB[jj % 2])
                    gate_due = None
                if n + 3 < nseq:
                    qk(n + 3)
                ex(n)
                pv(n)
                if pending_fin and n >= pending_fin[0][0] + FIN_DEFER:
                    pending_fin.pop(0)[1]()
                if d["last"]:
                    fin_a(d)
                    pending_fin.append((n, lambda d=d: fin_b(d)))
            for i in range(len(pending_fin)):
                pending_fin[i] = (-10**9, pending_fin[i][1])

        def gate_mm(wt, r_wt, j):
            pg, r_pg = ps[5], r_ps[5]
            mm8(pg[:], r_pg, wt, r_wt, lambda k: own_cols(k, j))

        def gate_ep(j, GBj, r_GBj):
            pg, r_pg = ps[5], r_ps[5]
            e_, r_e = f32t[j % 2], r_f32t[j % 2]
            B.op("act", lambda: nc.scalar.activation(out=e_[:, 0:512], in_=pg[:], func=AF.Exp, scale=-1.0), reads=[r_pg], writes=[r_e])
            B.op("dve", lambda: nc.vector.tensor_scalar(e_[:, 0:512], e_[:, 0:512], 1.0, None, ALU.add), reads=[r_e], writes=[r_e])
            B.op("dve", lambda: nc.vector.reciprocal(e_[:, 512:1024], e_[:, 0:512]), reads=[r_e], writes=[r_e])
            for hl in range(2):
                B.op("dve", lambda hl=hl: nc.vector.tensor_tensor(GBj[:, hl, :], pg[hl * 64:(hl + 1) * 64, :],
                                                                  e_[hl * 64:(hl + 1) * 64, 512:1024], ALU.mult),
                     reads=[r_pg, r_e], writes=[r_GBj])

        def evac_split(pa, r_pa, W, dst_fn, r_dsts, i):
            for hl in range(2):
                if hl == 0:
                    B.op("act", lambda hl=hl: nc.scalar.copy(out=dst_fn(hl), in_=pa[hl * 64:(hl + 1) * 64, 0:W]), reads=[r_pa], writes=[r_dsts[hl]])
                else:
                    B.op("dve", lambda hl=hl: nc.vector.tensor_copy(dst_fn(hl), pa[hl * 64:(hl + 1) * 64, 0:W]), reads=[r_pa], writes=[r_dsts[hl]])

        PB = [5, 7, 0, 1, 6]
        pb_state = {"i": 0}

        def next_pb():
            i = PB[pb_state["i"] % len(PB)]
            pb_state["i"] += 1
            return ps[i], r_ps[i]

        def v_proj(mm_fn):
            vi = 0
            for b0 in range(0, NB, 4):
                nb_ = min(4, NB - b0)
                pv_, r_pv = next_pb()
                for q in range(nb_):
                    mm_fn(pv_[:, q * 128:(q + 1) * 128], b0 + q, r_pv, q == nb_ - 1)
                dstv = VB[:, b0:b0 + nb_, 0:130].rearrange("p b (h c) -> p b h c", c=65)[:, :, :, 0:64]
                srcv = pv_[:, 0:nb_ * 128].rearrange("p (b h d) -> p b h d", b=nb_, h=2)
                if vi % 2 == 0:
                    B.op("act", lambda: nc.scalar.copy(out=dstv, in_=srcv), reads=[r_pv], writes=[r_VB2[0]])
                else:
                    B.op("dve", lambda: nc.vector.tensor_copy(dstv, srcv), reads=[r_pv], writes=[r_VB2[1]])
                vi += 1

        gstate = {"i": 0}

        def fox_weights(hp):
            for i in range(4):
                load_w8(wb[i], r_wb[i], w_fox[hp, :, :, i * 128:(i + 1) * 128])

        def phase_c_weights():
            for i in range(3):
                load_w8(wb[i], r_wb[i], w_c[:, :, (2 + i) * 128:(3 + i) * 128])

        fox_weights(0)
        for hp in range(4):
            for ti, (t0, W) in enumerate(TILES):
                pk, r_pk = next_pb()
                mm8(pk[:, 0:W], r_pk, wb[0], r_wb[0], lambda k: hT[:, k, t0:t0 + W])
                evac_split(pk, r_pk, W, lambda hl: KB[hl][0:64, t0:t0 + W], r_KB, ti)
            def fox_v_mm(out_ap, blk, r_bank, last):
                for k in range(8):
                    B.op("pe", lambda k=k: nc.tensor.matmul(out_ap, lhsT=hT[:, k, blk * 128:(blk + 1) * 128], rhs=wb[3][:, k, :],
                                                            start=(k == 0), stop=(k == 7)),
                         reads=[r_hT, r_wb[3]], writes=[r_bank], signal=(k == 7 and last))
            v_proj(fox_v_mm)
            for j in range(4):
                pq_, r_pq = next_pb()
                mm8(pq_[:], r_pq, wb[1], r_wb[1], lambda k: own_cols(k, j))
                evac_split(pq_, r_pq, 512, lambda hl: QB[hl][0:64, j * 512:(j + 1) * 512], r_QB, j)
            for hl in range(2):
                h = 2 * hp + hl
                B.dma("sp", "mv", QB[hl][64:65, :], cqr[96 + h:97 + h, 0:NOWN], reads=[r_cqr], writes=[r_QB[hl]])
            attention_pair(hp, "fox", wb[2], r_wb[2], hook=((lambda hp=hp: fox_weights(hp + 1)) if hp < 3 else phase_c_weights))

        r_ckvn, r_cqn = res("ckvn"), res("cqn")

        def norm_p1(psrc_list, W, par=0):
            single = (len(psrc_list) == 1)
            tmps = []
            for c, (pa, r_pa) in enumerate(psrc_list):
                if single:
                    tmp_ap, r_tmp = f32t[par][:, 0:W], r_f32t[par]
                    sq_ap, r_sq = bft[par][:, 0:W], r_bft[par]
                else:
                    tmp_ap, r_tmp = f32t[c][:, par * 512:par * 512 + W], r_f32t[c]
                    sq_ap, r_sq = bft[c][:, par * 512:par * 512 + W], r_bft[c]
                B.op("act", lambda pa=pa, tmp_ap=tmp_ap: nc.scalar.copy(out=tmp_ap, in_=pa), reads=[r_pa], writes=[r_tmp])
                B.op("dve", lambda pa=pa, tmp_ap=tmp_ap, sq_ap=sq_ap: nc.vector.tensor_tensor(sq_ap, pa, tmp_ap, ALU.mult),
                     reads=[r_pa, r_tmp], writes=[r_sq])
                tmps.append((tmp_ap, r_tmp, sq_ap, r_sq))
            return tmps

        def norm_p2(tmps, n_feat, gcols, dst_fn, r_dst, W, par=0):
            pq, r_pq = ps[4], r_ps[4]
            for c, (tmp_ap, r_tmp, sq_ap, r_sq) in enumerate(tmps):
                B.op("pe", lambda c=c, sq_ap=sq_ap: nc.tensor.matmul(pq[:, 0:W], lhsT=c_onesb[:], rhs=sq_ap,
                                                                     start=(c == 0), stop=(c == len(tmps) - 1)),
                     reads=[r_sq, r_const], writes=[r_pq], signal=(c == len(tmps) - 1))
            rr, r_rr = f32t[2 + par], r_f32t[2 + par]
            B.op("act", lambda: nc.scalar.activation(out=rr[:, 0:W], in_=pq[:, 0:W], func=AF.Ln, scale=1.0 / n_feat, bias=EPS),
                 reads=[r_pq], writes=[r_rr])
            B.op("act", lambda: nc.scalar.activation(out=rr[:, 0:W], in_=rr[:, 0:W], func=AF.Exp, scale=-0.5),
                 reads=[r_rr], writes=[r_rr])
            for c, (tmp_ap, r_tmp, sq_ap, r_sq) in enumerate(tmps):
                B.op("dve", lambda c=c, tmp_ap=tmp_ap: nc.vector.scalar_tensor_tensor(dst_fn(c), tmp_ap, gcols[:, c:c + 1], rr[:, 0:W],
                                                                                      ALU.mult, ALU.mult),
                     reads=[r_tmp, r_rr, r_const], writes=[r_dst])

        while pending_fin:
            pending_fin.pop(0)[1]()
        B.barrier()
        r_half = [[res("wst%d_h%d" % (i, h)) for h in range(2)] for i in range(2)]
        for ti, (t0, W) in enumerate(TILES):
            par = ti % 2
            b0, b1, b2 = (0, 1, 2) if par == 0 else (5, 6, 7)
            mm8(ps[b0][:, 0:W], r_ps[b0], wb[0], r_wb[0], lambda k: hT[:, k, t0:t0 + W])
            st_kv = norm_p1([(ps[b0][:, 0:W], r_ps[b0])], W, par)
            mm8(ps[b1][:, 0:W], r_ps[b1], wb[1], r_wb[1], lambda k: hT[:, k, t0:t0 + W])
            mm8(ps[b2][:, 0:W], r_ps[b2], wb[2], r_wb[2], lambda k: hT[:, k, t0:t0 + W])
            norm_p2(st_kv, 128.0, c_gkv, lambda c: ckvn[:, t0:t0 + W], r_ckvn, W, par)
            ct, r_ct = wst[0][:, par * 512:(par + 1) * 512], r_half[0][par]
            stt, r_stt = wst[1][:, par * 512:(par + 1) * 512], r_half[1][par]
            B.dma("sp", "tab", ct[64:96, 0:W], cck[:, t0:t0 + W], writes=[r_ct])
            B.dma("sp", "tab", stt[64:96, 0:W], ssk[:, t0:t0 + W], writes=[r_stt])
            B.op("dve", lambda: nc.vector.tensor_tensor(ct[64:96, 0:W], ps[b1][64:96, 0:W], ct[64:96, 0:W], ALU.mult),
                 reads=[r_ps[b1], r_ct], writes=[r_ct])
            B.op("dve", lambda: nc.vector.tensor_tensor(stt[64:96, 0:W], ps[b2][64:96, 0:W], stt[64:96, 0:W], ALU.mult),
                 reads=[r_ps[b2], r_stt], writes=[r_stt])
            for hl in range(2):
                B.op("dve", lambda hl=hl: nc.vector.tensor_tensor(KB[hl][64:96, t0:t0 + W], ct[64:96, 0:W], stt[64:96, 0:W], ALU.add),
                     reads=[r_ct, r_stt], writes=[r_KB[hl]])
        B.barrier()
        for i in range(2):
            load_w8(wb[i], r_wb[i], w_c[:, :, i * 128:(i + 1) * 128])
        r_wu = res("wu")
        r_wuq = res("wuq")
        for (dstw, srcw) in ((wukk, w_ukk), (wukv, w_ukv)):
            st, r_st = f32t[0], r_f32t[0]
            B.dma("sp", "w", st[:, 0:512], srcw[:, :], writes=[r_st])
            B.op("dve", lambda dstw=dstw, st=st: nc.vector.tensor_copy(dstw[:], st[:, 0:512]), reads=[r_st], writes=[r_wu])

        def mla_weights(hp):
            load_w8(wb[2], r_wb[2], w_mg[hp, :, :, :])
            for (dstw, srcw) in ((wuqn, w_uqn), (wuqr, w_uqr)):
                st, r_st = wst[1], r_wst[1]
                B.dma("sp", "w", st[:, 0:256].rearrange("p (c n) -> p c n", c=2), srcw[:, :, hp * 128:(hp + 1) * 128], writes=[r_st])
                B.op("dve", lambda dstw=dstw, st=st: nc.vector.tensor_copy(dstw[:].rearrange("p c n -> p (c n)"), st[:, 0:256]),
                     reads=[r_st], writes=[r_wuq])

        prev_q = None
        for j in range(4):
            par = j % 2
            b0, b1 = (0, 1) if par == 0 else (5, 6)
            mm8(ps[b0][:], r_ps[b0], wb[0], r_wb[0], lambda k: own_cols(k, j))
            mm8(ps[b1][:], r_ps[b1], wb[1], r_wb[1], lambda k: own_cols(k, j))
            st_q = norm_p1([(ps[b0][:], r_ps[b0]), (ps[b1][:], r_ps[b1])], 512, par)
            if prev_q is not None:
                norm_p2(prev_q[0], 256.0, c_gq, lambda c, jj=prev_q[1]: cqn[:, c, jj * 512:(jj + 1) * 512], r_cqn, 512, prev_q[2])
            prev_q = (st_q, j, par)
        norm_p2(prev_q[0], 256.0, c_gq, lambda c, jj=prev_q[1]: cqn[:, c, jj * 512:(jj + 1) * 512], r_cqn, 512, prev_q[2])

        r_wo = res("wo")

        def wo_view(k):
            if k < 4:
                return ckvn[:, k * 1024:(k + 1) * 1024]
            return cqn[:, (k - 4) // 2, ((k - 4) % 2) * 1024:((k - 4) % 2 + 1) * 1024]

        NXO = 8
        xo_slots = [(hT[:, k, 0:2048].bitcast(F32), res("xo%d" % k)) for k in range(NXO)]
        yo_slots = [(hT[:, k, 2048:4096].bitcast(F32), res("yo%d" % k)) for k in range(NXO)]
        o_state = {"hT_fenced": False}

        def load_xo(ob):
            xo, r_xo = xo_slots[ob % NXO]
            n0 = 1 + 2 * ob
            wr = [r_xo] + ([r_hT] if ob < NXO else [])
            B.dma("sp", "x", xo[0:64, :], xa[n0 * 128:n0 * 128 + 64, :], writes=wr)
            B.dma("sp", "x", xo[64:128, :], xa[(n0 + 1) * 128:(n0 + 1) * 128 + 64, :], writes=[r_xo])

        def phase_o_prefetch():
            for ob in range(NXO):
                load_xo(ob)

        mla_weights(0)
        for hp in range(4):
            for ti, (t0, W) in enumerate(TILES):
                pk, r_pk = next_pb()
                B.op("pe", lambda: nc.tensor.matmul(pk[:, 0:W], lhsT=wukk[:, hp * 128:(hp + 1) * 128], rhs=ckvn[:, t0:t0 + W], start=True, stop=True),
                     reads=[r_wu, r_ckvn], writes=[r_pk])
                evac_split(pk, r_pk, W, lambda hl: KB[hl][0:64, t0:t0 + W], r_KB, ti)
            def mla_v_mm(out_ap, blk, r_bank, last):
                B.op("pe", lambda: nc.tensor.matmul(out_ap, lhsT=ckvn[:, blk * 128:(blk + 1) * 128], rhs=wukv[:, hp * 128:(hp + 1) * 128],
                                                    start=True, stop=True),
                     reads=[r_wu, r_ckvn], writes=[r_bank], signal=last)
            v_proj(mla_v_mm)
            qsubs = [res("qrope_%s_%d" % (nm, p_)) for nm in ("t1", "t2", "t2b") for p_ in range(2)]
            B.op("dve", lambda: nc.vector.memset(small[:, 60:61], 0.0), writes=[r_f32t[0], r_f32t[1], res("qfence")] + qsubs)
            for j in range(4):
                qs = slice(j * 512, (j + 1) * 512)
                pn, r_pn = next_pb()
                pr, r_pr = next_pb()
                par = j % 2
                c0, c1 = par * 512, (1 - par) * 512
                r_t1, r_t2, r_t2b = res("qrope_t1_%d" % par), res("qrope_t2_%d" % par), res("qrope_t2b_%d" % par)
                t1 = f32t[0][0:64, c0:c0 + 512]
                t2 = f32t[1][64:128, c0:c0 + 512]
                t2b = f32t[1][0:64, c1:c1 + 512]
                B.dma("sp", "tab", t1, ccq[:, qs], writes=[r_t1])
                B.dma("sp", "tab", t2, ssq[:, qs], writes=[r_t2])
                for (pp, r_pp, wq) in ((pn, r_pn, wuqn), (pr, r_pr, wuqr)):
                    for c in range(2):
                        B.op("pe", lambda c=c, pp=pp, wq=wq: nc.tensor.matmul(pp[:], lhsT=wq[:, c, :], rhs=cqn[:, c, qs],
                                                                              start=(c == 0), stop=(c == 1)),
                             reads=[r_wuq, r_cqn], writes=[r_pp], signal=(c == 1))
                for hl in range(2):
                    B.op("act", lambda hl=hl: nc.scalar.copy(out=QB[hl][0:64, qs], in_=pn[hl * 64:(hl + 1) * 64, :]), reads=[r_pn], writes=[r_QB[hl]])
                B.op("dve", lambda: nc.vector.tensor_tensor(t1, pr[0:64, :], t1, ALU.mult),
                     reads=[r_pr, r_t1], writes=[r_t1])
                B.op("dve", lambda: nc.vector.tensor_tensor(t2b, pr[64:128, :], t2, ALU.mult),
                     reads=[r_pr, r_t2], writes=[r_t2b])
                for hl in range(2):
                    B.op("dve", lambda hl=hl: nc.vector.tensor_tensor(QB[hl][64:96, qs], f32t[0][hl * 32:(hl + 1) * 32, c0:c0 + 512],
                                                                      f32t[1][hl * 32:(hl + 1) * 32, c1:c1 + 512], ALU.add),
                         reads=[r_t1, r_t2b], writes=[r_QB[hl]])
            B.op("dve", lambda: nc.vector.memset(small[:, 60:61], 0.0), writes=[r_f32t[0], r_f32t[1], res("qfence")] + qsubs)
            if hp == 3:
                for k in range(8):
                    i = k % 2
                    B.dma("sp", "w", wst[i][:], w_out[:, k, :], writes=[r_wst[i]])
                    B.op("dve", lambda k=k, i=i: nc.vector.tensor_copy(wo_view(k), wst[i][:]), reads=[r_wst[i]], writes=[r_wo, r_ckvn if k < 4 else r_cqn])
            attention_pair(4 + hp, "mla", wb[2], r_wb[2], hook=((lambda hp=hp: mla_weights(hp + 1)) if hp < 3 else phase_o_prefetch))

        while pending_fin:
            pending_fin.pop(0)[1]()
        gp, r_gp = wst[0], r_wst[0]
        B.dma("sp", "c", gp[:], gpost[:, :], writes=[r_gp])
        for ob in range(16):
            q4 = ob % 4
            pyA, r_pyA = ps[q4 * 2], r_ps[q4 * 2]
            pyB, r_pyB = ps[q4 * 2 + 1], r_ps[q4 * 2 + 1]
            for (py, r_py, c0) in ((pyA, r_pyA, 0), (pyB, r_pyB, 512)):
                for c in range(8):
                    B.op("pe", lambda c=c, py=py, c0=c0: nc.tensor.matmul(py[:], lhsT=attnT[:, c, ob * 128:(ob + 1) * 128], rhs=wo_view(c)[:, c0:c0 + 512],
                                                                          start=(c == 0), stop=(c == 7)),
                         reads=[r_attn, r_wo], writes=[r_py], signal=(c == 7))
            junk, r_junk = bft[ob % 2], r_bft[ob % 2]
            o4 = 4 * q4
            ssA, ssB, rO, scrO = (small[:, o4 + q:o4 + q + 1] for q in range(4))
            r_s = res("sO%d" % q4)
            B.op("act", lambda: nc.scalar.activation(out=junk[:, 0:512], in_=pyA[:], func=AF.Square, accum_out=ssA),
                 reads=[r_pyA], writes=[r_junk, r_s])
            B.op("act", lambda: nc.scalar.activation(out=junk[:, 512:1024], in_=pyB[:], func=AF.Square, accum_out=ssB),
                 reads=[r_pyB], writes=[r_junk, r_s])
            B.op("act", lambda: nc.scalar.copy(out=small[:, 62:63], in_=small[:, 63:64]), reads=[res("small_init")], writes=[r_s, res("fence")])
            B.op("pool", lambda: nc.gpsimd.tensor_tensor(ssA, ssA, ssB, ALU.add), reads=[r_s], writes=[r_s])
            rsqrt_pool(rO, ssA, float(D), scrO, [r_s], [r_s], r_s)
            xo, r_xo = xo_slots[ob % NXO]
            yo, r_yo = yo_slots[ob % NXO]
            wy = [r_yo] + ([r_hT] if ob < NXO else [])
            B.op("dve", lambda: nc.vector.scalar_tensor_tensor(yo[:, 0:512], pyA[:], rO, gp[:, 0:512], ALU.mult, ALU.mult),
                 reads=[r_pyA, r_s, r_gp], writes=wy)
            B.op("dve", lambda: nc.vector.scalar_tensor_tensor(yo[:, 512:1024], pyB[:], rO, gp[:, 512:1024], ALU.mult, ALU.mult),
                 reads=[r_pyB, r_s, r_gp], writes=[r_yo])
            B.op("dve", lambda: nc.vector.tensor_tensor(yo, yo, xo, ALU.add), reads=[r_xo, r_yo], writes=[r_yo])
            if ob + NXO < 16:
                load_xo(ob + NXO)
            B.dma("sp", "out", yout[ob * 128:(ob + 1) * 128, :], yo, reads=[r_yo])
        B._wait("sp", [(k, v) for k, v in B.cnt.items() if k.startswith("dma") and v > 0])
        print("build: counts", B.cnt, "waits", B.nwaits, "sbuf left", nc.sbuf_bytes_remaining)
    return nc


def _bf16(a):
    return np.ascontiguousarray(a.astype(ml_dtypes.bfloat16))


def _rope_tables(pos):
    inv = (10000.0 ** (-np.arange(0, 32, 2, dtype=np.float32) / np.float32(32))).astype(np.float32)
    ang = pos.astype(np.float32)[:, None] * inv[None, :]
    c, s = np.cos(ang).astype(np.float32), np.sin(ang).astype(np.float32)
    cc = np.concatenate([c, c], axis=1).T
    ss = np.concatenate([-s, s], axis=1).T
    return np.ascontiguousarray(cc), np.ascontiguousarray(ss)


def _k8(w):
    return np.ascontiguousarray(w.reshape(8, 128, -1).transpose(1, 0, 2))


def make_inputs(x, meta, norm_pre, norm_post, w_in, b_f, q_norm, w_uq, kv_norm, w_ukv, w_out):
    x = np.asarray(x, np.float32)
    w_in = np.asarray(w_in, np.float32)[0]
    w_uq = np.asarray(w_uq, np.float32)[0]
    w_ukv = np.asarray(w_ukv, np.float32)[0]
    w_out = np.asarray(w_out, np.float32)[0]
    common = {}
    common["w_fl"] = _k8(w_in[:, O_FL:O_FL + 8])
    common["bfb"] = np.ascontiguousarray(np.tile(np.asarray(b_f, np.float32)[0][None, :], (128, NB)))
    npre = np.asarray(norm_pre, np.float32)[0].reshape(8, 128).T
    common["npre"] = np.ascontiguousarray(npre)
    z64 = np.zeros((D, 64), np.float32)
    z32 = np.zeros((D, 32), np.float32)
    kr = w_in[:, O_KR:O_KR + 32]
    krs = np.concatenate([kr[:, 16:32], kr[:, 0:16]], axis=1)
    wc = np.concatenate([w_in[:, O_CQ:O_CQ + 256], w_in[:, O_CKV:O_CKV + 128], z64, kr, z32, z64, krs, z32], axis=1)
    common["w_c"] = _k8(wc)
    wf = []
    for hp in range(4):
        sl = slice(hp * 128, (hp + 1) * 128)
        wf.append(_k8(np.concatenate([w_in[:, O_FK:O_FK + 512][:, sl], w_in[:, O_FQ:O_FQ + 512][:, sl],
                                      w_in[:, O_FG:O_FG + 512][:, sl], w_in[:, O_FV:O_FV + 512][:, sl]], axis=1)))
    common["w_fox"] = np.ascontiguousarray(np.stack(wf))
    common["w_mg"] = np.ascontiguousarray(np.stack([_k8(w_in[:, O_MG + hp * 128:O_MG + (hp + 1) * 128]) for hp in range(4)]))
    wq = w_uq.reshape(256, 8, 96)

    def _k2(w):
        return np.ascontiguousarray(w.reshape(2, 128, -1).transpose(1, 0, 2))
    common["w_uqn"] = _k2(wq[:, :, 0:64].reshape(256, 512))
    ra = wq[:, :, 64:96].reshape(256, 4, 64)
    rb = np.concatenate([wq[:, :, 80:96], wq[:, :, 64:80]], axis=2).reshape(256, 4, 64)
    common["w_uqr"] = _k2(np.concatenate([ra, rb], axis=2).reshape(256, 512))
    wkv = w_ukv.reshape(128, 8, 128)
    common["w_ukk"] = np.ascontiguousarray(wkv[:, :, 0:64].reshape(128, 512))
    common["w_ukv"] = np.ascontiguousarray(wkv[:, :, 64:128].reshape(128, 512))
    common["w_out"] = _k8(w_out)
    common["gq"] = np.ascontiguousarray(np.asarray(q_norm, np.float32)[0].reshape(2, 128).T)
    common["gkv"] = np.ascontiguousarray(np.asarray(kv_norm, np.float32)[0].reshape(128, 1))
    common["gpost"] = np.ascontiguousarray(np.tile(np.asarray(norm_post, np.float32)[0][None, :], (128, 1)))
    colc = np.zeros((128, 4), np.float32)
    colc[16:, 0] = NEG
    colc[:16, 1] = 1.0
    common["colc"] = colc
    common["identb"] = _bf16(np.eye(128, dtype=np.float32))
    common["identf"] = np.eye(128, dtype=np.float32)
    meta = np.asarray(meta, np.float32)
    in_maps = []
    r = np.arange(128)
    for core in range(8):
        b, g = core // 2, core % 2
        m = dict(common)
        xa = np.zeros((NB, 128, D), np.float32)
        xa[0, :16] = meta
        xc = x[b].reshape(32, 2, 64, D)
        xa[1:, 0:64] = xc[:, g]
        xa[1:, 64:128] = xc[:, 1 - g]
        m["xa"] = xa.reshape(TA, D)
        tpos = (r + 64 * g) % 128 if g == 1 else r
        if g == 1:
            tpos = np.where(r < 64, r + 64, r - 64)
        m["umat"] = np.ascontiguousarray((tpos[:, None] <= tpos[None, :]).astype(np.float32))
        mf = np.full((128, 64), NEG, np.float32)
        mm_ = np.full((128, 64), NEG, np.float32)
        cq_ = np.arange(64)
        own_ok = (r[:64, None] <= cq_[None, :])
        mf[:64] = np.where(own_ok, 0.0, NEG)
        mm_[:64] = 0.0
        if g == 1:
            mf[64:] = 0.0
            mm_[64:] = 0.0
        m["maskf"] = _bf16(mf)
        m["maskm"] = _bf16(mm_)
        pos = np.zeros((NB, 128), np.float32)
        pos[0] = np.arange(128)
        nn = np.arange(32)[:, None]
        rr = np.arange(128)[None, :]
        chunk = np.where(rr < 64, 2 * nn + g, 2 * nn + 1 - g)
        pos[1:] = 16 + 64 * chunk + (rr % 64)
        cc, ss = _rope_tables(pos.reshape(-1))
        m["cck"], m["ssk"] = cc, ss
        posq = pos[1:, 0:64].reshape(-1)
        cc, ss = _rope_tables(posq)
        m["ccq"] = np.ascontiguousarray(np.concatenate([cc, cc], axis=0))
        m["ssq"] = np.ascontiguousarray(np.concatenate([ss, ss], axis=0))
        in_maps.append(m)
    return in_maps


_CACHE = {}


def kernel(x, meta, norm_pre, norm_post, w_in, b_f, q_norm, w_uq, kv_norm, w_ukv, w_out):
    if "nc" not in _CACHE:
        _CACHE["nc"] = build_program()
    nc = _CACHE["nc"]
    in_maps = make_inputs(x, meta, norm_pre, norm_post, w_in, b_f, q_norm, w_uq, kv_norm, w_ukv, w_out)
    res = run_bass_kernel_spmd(nc, in_maps, core_ids=list(range(8)))
    out = np.zeros((4, 32, 2, 64, D), np.float32)
    for core in range(8):
        b, g = core // 2, core % 2
        out[b, :, g] = np.asarray(res.results[core]["yout"], np.float32).reshape(32, 64, D)
    return out.reshape(4, S, D)
```
